# Optimizing a Trainium2 kernel written in Bass

```python
import math
import jax, jax.numpy as jnp
from jax import lax
import numpy as np

D_MODEL = 4096
BATCH = 4
SEQ = 4096
DEPTH = 2

CHUNK = 64
N_META = 16
Q_BLOCK = 128
MIX_WIDTH = D_MODEL
A_WIDTH = MIX_WIDTH // 2
B_WIDTH = MIX_WIDTH - A_WIDTH
A_DK = 128
A_HEADS = A_WIDTH // A_DK
A_DV = A_WIDTH // A_HEADS
B_DV = 128
B_HEADS = B_WIDTH // B_DV
B_DH = B_DV // 2
D_FF = 4 * D_MODEL
IN_COLS = 4 * A_WIDTH + 2 * (B_HEADS * 2 * B_DH) + B_HEADS * B_DV
EPS = 1e-6
MASK_VALUE = -1e30

kernel_name = "hymba_hgrn2_diffattn_chunk_causal"


def rms_norm(x, g):
    xf = x.astype(jnp.float32)
    y = xf * lax.rsqrt(jnp.mean(xf * xf, axis=-1, keepdims=True) + EPS)
    return (y * g.astype(jnp.float32)).astype(x.dtype)


def chunk_id(pos):
    return jnp.where(pos < N_META, 0, (pos - N_META) // CHUNK + 1)


def alibi_slopes(n_heads):
    h = jnp.arange(1, n_heads + 1, dtype=jnp.float32)
    return jnp.exp2(-8.0 * h / n_heads)


def hgrn2_chunk_step(S, inp):
    q, k, v, logf = inp
    C = q.shape[1]
    b = jnp.cumsum(logf, axis=1)
    causal = jnp.tril(jnp.ones((C, C), dtype=bool))
    diff = b[:, :, None] - b[:, None, :]
    decay = jnp.exp(jnp.where(causal[None, :, :, None, None], diff, -jnp.inf))
    A = jnp.einsum('bthd,bshd,btshd->bhts', q, k, decay)
    o_intra = jnp.einsum('bhts,bshe->bthe', A, v)
    o_inter = jnp.einsum('bthd,bhde->bthe', q * jnp.exp(b), S)
    b_last = b[:, -1]
    k_dec = k * jnp.exp(b_last[:, None] - b)
    S_new = jnp.exp(b_last)[..., None] * S + jnp.einsum('bshd,bshe->bhde', k_dec, v)
    return S_new, o_intra + o_inter


def hgrn2_mixer(q, fpre, i, g, lb, out_g):
    Bn, L, _ = q.shape
    f32 = jnp.float32
    qh = q.astype(f32).reshape(Bn, L, A_HEADS, A_DK)
    fh = fpre.astype(f32).reshape(Bn, L, A_HEADS, A_DK)
    vh = i.astype(f32).reshape(Bn, L, A_HEADS, A_DV)
    lbh = lb.reshape(A_HEADS, A_DK)
    f = lbh + (1.0 - lbh) * jax.nn.sigmoid(fh)
    logf = jnp.log(f)
    kh = (1.0 - lbh) * jax.nn.sigmoid(-fh)
    S0 = jnp.zeros((Bn, A_HEADS, A_DK, A_DV), f32)
    S1, o_meta = hgrn2_chunk_step(S0, (qh[:, :N_META], kh[:, :N_META], vh[:, :N_META], logf[:, :N_META]))
    n_chunks = (L - N_META) // CHUNK

    def to_chunks(t):
        t = t[:, N_META:]
        t = t.reshape(Bn, n_chunks, CHUNK, *t.shape[2:])
        return jnp.moveaxis(t, 1, 0)

    _, o_real = lax.scan(hgrn2_chunk_step, S1, (to_chunks(qh), to_chunks(kh), to_chunks(vh), to_chunks(logf)))
    o_real = jnp.moveaxis(o_real, 0, 1).reshape(Bn, L - N_META, A_HEADS, A_DV)
    o = jnp.concatenate([o_meta, o_real], axis=1)
    o = rms_norm(o, out_g).reshape(Bn, L, A_WIDTH)
    return (o * jax.nn.silu(g.astype(f32))).astype(q.dtype)


def diff_attention(q, k, v, lam, lam_init, out_g):
    Bn, L = q.shape[0], q.shape[1]
    pos = jnp.arange(L)
    kchunk = chunk_id(pos)
    slopes = alibi_slopes(B_HEADS)
    scale = 1.0 / math.sqrt(B_DH)

    def attend(qb, qposb):
        s = jnp.einsum('bqhcd,bkhcd->bchqk', qb, k).astype(jnp.float32) * scale
        dist = jnp.abs(qposb[:, None] - pos[None, :]).astype(jnp.float32)
        s = s - slopes[:, None, None] * dist
        mask = kchunk[None, :] <= chunk_id(qposb)[:, None]
        s = jnp.where(mask, s, MASK_VALUE)
        p = jax.nn.softmax(s, axis=-1)
        a = p[:, 0] - lam * p[:, 1]
        return jnp.einsum('bhqk,bkhe->bqhe', a.astype(v.dtype), v)

    o_meta = attend(q[:, :N_META], pos[:N_META])
    n_blk = (L - N_META) // Q_BLOCK
    qr = jnp.moveaxis(q[:, N_META:].reshape(Bn, n_blk, Q_BLOCK, B_HEADS, 2, B_DH), 1, 0)
    pr = pos[N_META:].reshape(n_blk, Q_BLOCK)
    o_real = lax.map(lambda a: attend(a[0], a[1]), (qr, pr))
    o_real = jnp.moveaxis(o_real, 0, 1).reshape(Bn, L - N_META, B_HEADS, B_DV)
    o = jnp.concatenate([o_meta, o_real], axis=1)
    o = rms_norm(o, out_g) * (1.0 - lam_init)
    return o.reshape(Bn, L, B_WIDTH).astype(q.dtype)


def setup_inputs(seed: int = 0) -> dict:
    key = jax.random.key(seed)
    ks = jax.random.split(key, 16)
    f32 = jnp.float32
    nrm = lambda k, shape, std: jax.random.normal(k, shape, f32) * std
    return {
        "x": nrm(ks[0], (BATCH, SEQ, D_MODEL), 1.0),
        "meta_tokens": nrm(ks[1], (N_META, D_MODEL), 1.0),
        "norm1_g": 1.0 + nrm(ks[2], (DEPTH, D_MODEL), 0.01),
        "w_in": nrm(ks[3], (DEPTH, D_MODEL, IN_COLS), D_MODEL ** -0.5),
        "hgrn_lb_raw": nrm(ks[4], (DEPTH, A_WIDTH), 0.1),
        "hgrn_out_g": 1.0 + nrm(ks[5], (DEPTH, A_DV), 0.01),
        "q_norm_g": 1.0 + nrm(ks[6], (DEPTH, 2, B_DH), 0.01),
        "k_norm_g": 1.0 + nrm(ks[7], (DEPTH, 2, B_DH), 0.01),
        "diff_lambda": nrm(ks[8], (DEPTH, 4, B_DH), 0.1),
        "diff_out_g": 1.0 + nrm(ks[9], (DEPTH, B_DV), 0.01),
        "w_out": nrm(ks[10], (DEPTH, MIX_WIDTH, D_MODEL), MIX_WIDTH ** -0.5),
        "norm2_g": 1.0 + nrm(ks[11], (DEPTH, D_MODEL), 0.01),
        "w_mlp_up": nrm(ks[12], (DEPTH, D_MODEL, D_FF), D_MODEL ** -0.5),
        "w_mlp_down": nrm(ks[13], (DEPTH, D_FF, D_MODEL), 0.5 * D_FF ** -0.5),
    }


def reference(x, meta_tokens, norm1_g, w_in, hgrn_lb_raw, hgrn_out_g, q_norm_g, k_norm_g,
              diff_lambda, diff_out_g, w_out, norm2_g, w_mlp_up, w_mlp_down):
    Bn = x.shape[0]
    meta = jnp.broadcast_to(meta_tokens.astype(x.dtype)[None], (Bn, N_META, D_MODEL))
    h = jnp.concatenate([meta, x], axis=1)
    L = h.shape[1]
    lb_all = jnp.cumsum(jax.nn.softmax(hgrn_lb_raw.astype(jnp.float32), axis=0), axis=0)
    lb_all = lb_all - lb_all[0:1]
    split_at = np.cumsum([A_WIDTH, A_WIDTH, A_WIDTH, A_WIDTH,
                          B_HEADS * 2 * B_DH, B_HEADS * 2 * B_DH])
    for layer in range(DEPTH):
        u = rms_norm(h, norm1_g[layer])
        proj = u @ w_in[layer]
        a_q, a_f, a_i, a_g, b_q, b_k, b_v = jnp.split(proj, split_at, axis=-1)
        o_a = hgrn2_mixer(a_q, a_f, a_i, a_g, lb_all[layer], hgrn_out_g[layer])
        bq = rms_norm(b_q.reshape(Bn, L, B_HEADS, 2, B_DH), q_norm_g[layer])
        bk = rms_norm(b_k.reshape(Bn, L, B_HEADS, 2, B_DH), k_norm_g[layer])
        bv = b_v.reshape(Bn, L, B_HEADS, B_DV)
        lp = diff_lambda[layer].astype(jnp.float32)
        lam_init = 0.8 - 0.6 * math.exp(-0.3 * layer)
        lam = jnp.exp(jnp.sum(lp[0] * lp[1])) - jnp.exp(jnp.sum(lp[2] * lp[3])) + lam_init
        o_b = diff_attention(bq, bk, bv, lam, lam_init, diff_out_g[layer])
        h = h + jnp.concatenate([o_a, o_b], axis=-1) @ w_out[layer]
        u = rms_norm(h, norm2_g[layer])
        z = jax.nn.relu(u @ w_mlp_up[layer])
        h = h + (z * z) @ w_mlp_down[layer]
    return h[:, N_META:]
```

```python
import math
import numpy as np
import concourse.bass as bass
import concourse.mybir as mybir
from concourse.bass_utils import run_bass_kernel_spmd

F32 = mybir.dt.float32
BF16 = mybir.dt.bfloat16
AF = mybir.ActivationFunctionType
ALU = mybir.AluOpType
AX = mybir.AxisListType

EPS = 1e-6
N_META = 16


class Cfg:
    def __init__(self, D=4096, SEQ=4096, HA=16, HB=16, DFF=16384, DEPTH=2, GROUP=8, PARTC=16):
        self.D, self.SEQ, self.HA, self.HB, self.DFF, self.DEPTH = D, SEQ, HA, HB, DFF, DEPTH
        self.KC = D // 128
        self.NT = 1 + SEQ // 128
        self.LP = self.NT * 128
        self.AW = HA * 128
        self.BW = HB * 128
        self.MW = self.AW + self.BW
        self.INC = 4 * self.AW + 3 * self.BW
        self.PMW = 2 * self.AW + 3 * self.BW
        self.GROUP = GROUP
        self.PARTC = min(PARTC, DFF // 128)
        gs = []
        t = 0
        first = True
        while t < self.NT:
            n = GROUP + 1 if first else GROUP
            gs.append(list(range(t, min(self.NT, t + n))))
            t += n
            first = False
        self.groups = gs
        self.TGMAX = max(len(g) for g in gs) * 128


class Sem:
    __slots__ = ("h", "name")

    def __init__(self, h, name):
        self.h = h
        self.name = name


class Buf:
    __slots__ = ("t", "w", "r", "dsem", "dcnt", "name")

    def __init__(self, t, name):
        self.t = t
        self.name = name
        self.w = {}
        self.r = {}
        self.dsem = None
        self.dcnt = 0

    def __getitem__(self, k):
        return self.t[k]


class Eng:
    def __init__(self, K, name, h):
        self.K = K
        self.name = name
        self.h = h
        self.sem = K.new_sem("e_" + name)
        self.cnt = 0
        self.seen = {}

    def wait_tokens(self, toks):
        for s, v in toks.items():
            if s is self.sem and self.name == "pe":
                continue
            if self.seen.get(s, 0) >= v:
                continue
            self.h.wait_ge(s.h, v)
            self.seen[s] = v


class Kern:
    def __init__(self, nc):
        self.nc = nc
        self.nsem = 0
        self.pe = Eng(self, "pe", nc.tensor)
        self.act = Eng(self, "act", nc.scalar)
        self.dve = Eng(self, "dve", nc.vector)
        self.pool = Eng(self, "pool", nc.gpsimd)
        self.sp = Eng(self, "sp", nc.sync)
        self.engs = [self.pe, self.act, self.dve, self.pool, self.sp]
        self.store_toks = {}
        self.all_dma = {}
        self.free_sems = []
        self.phase_bufs = []

    def new_sem(self, name):
        self.nsem += 1
        return Sem(self.nc.alloc_semaphore(name=f"{name}_{self.nsem}"), name)

    @staticmethod
    def _merge(d, s):
        for k, v in s.items():
            if d.get(k, 0) < v:
                d[k] = v

    def op(self, eng, reads, writes, fn):
        toks = {}
        for b in reads:
            self._merge(toks, b.w)
        for b in writes:
            self._merge(toks, b.w)
            self._merge(toks, b.r)
        eng.wait_tokens(toks)
        ins = fn(eng.h)
        eng.cnt += 1
        ins.then_inc(eng.sem.h, 1)
        for b in reads:
            b.r[eng.sem] = eng.cnt
        for b in writes:
            b.w[eng.sem] = eng.cnt
        return ins

    def dma(self, q, dst, src, dst_ap, src_ap, sb, is_store=False):
        toks = {}
        self._merge(toks, src.w)
        self._merge(toks, dst.w)
        self._merge(toks, dst.r)
        q.wait_tokens(toks)
        ins = q.h.dma_start(out=dst_ap, in_=src_ap)
        self.get_dsem(sb)
        sb.dcnt += 16
        ins.then_inc(sb.dsem.h, 16)
        src.r[sb.dsem] = sb.dcnt
        dst.w[sb.dsem] = sb.dcnt
        self.all_dma[sb.dsem] = sb.dcnt
        if is_store:
            self.store_toks[sb.dsem] = sb.dcnt
        return ins

    def get_dsem(self, sb):
        if sb.dsem is None:
            if self.free_sems:
                sb.dsem, sb.dcnt = self.free_sems.pop()
            else:
                sb.dsem = self.new_sem("d")
            self.phase_bufs.append(sb)

    def barrier(self, release=True):
        toks = {}
        for e in self.engs:
            if e.cnt:
                toks[e.sem] = e.cnt
        self._merge(toks, self.all_dma)
        for e in self.engs:
            e.wait_tokens(toks)
        if release:
            for b in self.phase_bufs:
                self.free_sems.append((b.dsem, b.dcnt))
            self.phase_bufs = []

    def store_barrier(self, dram_bufs):
        for b in dram_bufs:
            self._merge(b.w, self.store_toks)


from contextlib import ExitStack


def lam_init_of(layer):
    return 0.8 - 0.6 * math.exp(-0.3 * layer)


def alibi_slope(h, n):
    return float(2.0 ** (-8.0 * (h + 1) / n))


def build_program(cfg, debug=None, phases=None):
    c = cfg
    nc = bass.Bass("TRN2", target_bir_lowering=False)
    D, KC, NT, LP = c.D, c.KC, c.NT, c.LP
    NCH = c.SEQ // 64
    NQB = c.SEQ // 512
    debug = debug or []

    def din(name, shape, dtype=F32):
        return nc.dram_tensor(name, list(shape), dtype, kind="ExternalInput").ap()

    x = din("x", [c.SEQ, D])
    meta = din("meta_tokens", [N_META, D])
    norm1_g = din("norm1_g", [c.DEPTH, D])
    w_in = din("w_in", [c.DEPTH, D, c.INC])
    lb_raw = din("hgrn_lb_raw", [c.DEPTH, c.AW])
    hgrn_out_g = din("hgrn_out_g", [c.DEPTH, 128])
    q_norm_g = din("q_norm_g", [c.DEPTH, 128])
    k_norm_g = din("k_norm_g", [c.DEPTH, 128])
    diff_lambda = din("diff_lambda", [c.DEPTH, 256])
    diff_out_g = din("diff_out_g", [c.DEPTH, 128])
    w_out = din("w_out", [c.DEPTH, c.MW, D])
    norm2_g = din("norm2_g", [c.DEPTH, D])
    w_up = din("w_mlp_up", [c.DEPTH, D, c.DFF])
    w_dn = din("w_mlp_down", [c.DEPTH, c.DFF, D])
    cst = din("cst", [128, 256])
    rmask = din("rmask", [1, LP])
    abias = din("abias", [c.HB, 5, 128, 512])
    out = nc.dram_tensor("out", [c.SEQ, D], F32, kind="ExternalOutput").ap()

    def dscr(name, shape, dtype):
        kind = "ExternalOutput" if name in debug else "Internal"
        return nc.dram_tensor(name, list(shape), dtype, kind=kind).ap()

    h_d = dscr("h_scr", [LP, D], F32)
    pT_d = dscr("pT_scr", [2 * c.AW, LP], F32)
    pM_d = dscr("pM_scr", [LP, c.PMW], F32)
    oT_d = dscr("oT_scr", [c.MW, LP], BF16)

    K = Kern(nc)
    pe, act, dve, pool, sp = K.pe, K.act, K.dve, K.pool, K.sp
    es_top = ExitStack()

    nmc = {"n": 0}

    def sb(es, name, shape, dtype):
        nmc["n"] += 1
        name = f"{name}_{nmc['n']}"
        t = es.enter_context(nc.sbuf_tensor(name, list(shape), dtype))
        return Buf(t, name)

    Bx = Buf(None, "x")
    Bh = Buf(None, "h")
    BpT = Buf(None, "pT")
    BpM = Buf(None, "pM")
    BoT = Buf(None, "oT")
    Bout = Buf(None, "out")
    hpiece = {}

    def hbuf(key):
        if key not in hpiece:
            hpiece[key] = Buf(None, "hp")
        return hpiece[key]

    ps = [Buf(es_top.enter_context(nc.psum_tensor(f"ps{i}", [128, 512], F32)), f"ps{i}") for i in range(8)]

    cst_sb = sb(es_top, "cst_sb", [128, 256], F32)
    K.dma(sp, cst_sb, Bx, cst_sb[:, :], cst[:, :], cst_sb)
    identb = sb(es_top, "identb", [128, 128], BF16)
    K.op(dve, [cst_sb], [identb], lambda e: e.tensor_copy(out=identb[:, :], in_=cst_sb[:, 0:128]))
    ident = cst_sb.t[:, 0:128]
    tri = cst_sb.t[:, 128:256]

    SW = 256
    KCMAX = max(KC, c.MW // 128, c.PARTC)
    SKC = 4
    wslab = []
    wstage = []
    wctr = {"slab": 0, "stage": 0}

    def alloc_w(es, nstage=2):
        wslab[:] = [sb(es, f"wslab{i}", [128, KCMAX, SW], BF16) for i in range(2)]
        wstage[:] = [sb(es, f"wstage{i}", [128, SKC, SW], F32) for i in range(nstage)]

    NSID = max(c.INC // SW, D // SW, (c.PARTC * 128) // SW * (c.DFF // (c.PARTC * 128)) + (D // SW) * (c.DFF // (c.PARTC * 128)))
    wbf_l = [nc.dram_tensor(f"wbf_scr{i}", [64, 128, KCMAX * SW], BF16, kind="Internal").ap()
             for i in range((NSID + 63) // 64)]

    class _WB:
        def __getitem__(self, k):
            sid = k[0]
            return wbf_l[sid // 64][(sid % 64,) + tuple(k[1:])]
    wbf_d = _WB()
    wbuf = {}

    def wb(sid):
        if sid not in wbuf:
            wbuf[sid] = Buf(None, "wb")
        return wbuf[sid]

    def load_slab(la, gi, sid):
        wap, k0, nk, c0, ncols = la
        slab = wslab[wctr["slab"] % len(wslab)]
        wctr["slab"] += 1
        if gi > 0:
            K.dma(sp, slab, wb(sid), slab[:, 0:nk, :],
                  wbf_d[sid, :, 0:nk * SW].rearrange("p (k n) -> p k n", n=SW), slab)
            return slab
        for kk in range(0, nk, SKC):
            n = min(SKC, nk - kk)
            st = wstage[wctr["stage"] % len(wstage)]
            wctr["stage"] += 1
            src = wap[(k0 + kk) * 128:(k0 + kk + n) * 128, c0:c0 + ncols].rearrange("(k p) n -> p k n", p=128)
            K.dma(sp, st, Bx, st[:, 0:n, 0:ncols], src, st)
            K.op(pool, [st], [slab], lambda e, st=st, slab=slab, kk=kk, n=n:
                 e.tensor_copy(out=slab[:, kk:kk + n, 0:ncols], in_=st[:, 0:n, 0:ncols]))
        if len(c.groups) > 1:
            K.dma(pool, wb(sid), slab, wbf_d[sid, :, 0:nk * SW].rearrange("p (k n) -> p k n", n=SW),
                  slab[:, 0:nk, :], slab, is_store=True)
        return slab

    def run_jobs(jobs, gi):
        if gi == 0:
            for sid, (la, fn) in enumerate(jobs):
                fn(load_slab(la, 0, sid))
            return
        cur = load_slab(jobs[0][0], gi, 0)
        for i, (la, fn) in enumerate(jobs):
            nxt = load_slab(jobs[i + 1][0], gi, i + 1) if i + 1 < len(jobs) else None
            fn(cur)
            cur = nxt

    evc = {"n": 0}

    def rstd_op(o, i, n, bufs):
        K.op(dve, bufs, bufs, lambda e: e.tensor_scalar(out=o, in0=i, scalar1=1.0 / n, scalar2=EPS,
                                                        op0=ALU.mult, op1=ALU.add))
        K.op(act, bufs, bufs, lambda e: e.sqrt(out=o, in_=o))
        K.op(dve, bufs, bufs, lambda e: e.reciprocal(out=o, in_=o))

    def rms_bufs(es, name, nb=2):
        ht = [sb(es, f"{name}_ht{i}", [128, D], F32) for i in range(nb)]
        ub = [sb(es, f"{name}_ub{i}", [128, D], BF16) for i in range(nb)]
        st = [sb(es, f"{name}_st{i}", [128, 2], F32) for i in range(nb)]
        gt = sb(es, f"{name}_gt", [128, D], F32)
        return ht, ub, st, gt

    def rms_to_xT(rb, tiles, src_of_tile, g_ap, xT, first):
        ht, ub, st, gt = rb
        if first:
            K.dma(sp, gt, Bx, gt[:, :], g_ap.partition_broadcast(128), gt)
        for i, t in enumerate(tiles):
            hb, u, s = ht[i % len(ht)], ub[i % len(ht)], st[i % len(ht)]
            src_of_tile(t, hb)
            K.op(dve, [], [s], lambda e: e.memset(s[:, 0:1], 0.0))
            K.op(act, [hb], [u, s], lambda e: e.activation(out=u[:, :], in_=hb[:, :], func=AF.Square,
                                                           accum_out=s[:, 0:1]))
            rstd_op(s[:, 1:2], s[:, 0:1], float(D), [s])
            K.op(dve, [hb, s, gt], [u], lambda e: e.scalar_tensor_tensor(
                out=u[:, :], in0=hb[:, :], scalar=s[:, 1:2], in1=gt[:, :], op0=ALU.mult, op1=ALU.mult))
            for k8 in range(0, KC, 8):
                n = min(8, KC - k8)
                pb = ps[evc["n"] % 2]
                evc["n"] += 1
                pv = pb.t[:, :].bitcast(BF16)

                def tr(e, u=u, pv=pv, k8=k8, n=n):
                    ins = None
                    for j in range(n):
                        ins = e.transpose(out=pv[:, j * 128:(j + 1) * 128],
                                          in_=u[:, (k8 + j) * 128:(k8 + j + 1) * 128], identity=identb[:, :])
                    return ins
                K.op(pe, [u, identb], [pb], tr)
                K.op(act, [pb], [xT], lambda e, pv=pv, k8=k8, n=n, i=i: e.copy(
                    out=xT[:, k8:k8 + n, i * 128:(i + 1) * 128],
                    in_=pv[:, 0:n * 128].rearrange("p (a b) -> p a b", b=128)))

    def h_src(layer):
        def f(t, hb):
            if layer == 0:
                if t == 0:
                    K.op(pool, [], [hb], lambda e: e.memset(hb[:, :], 0.0))
                    K.dma(sp, hb, Bx, hb[0:N_META, :], meta[:, :], hb)
                else:
                    K.dma(sp, hb, Bx, hb[:, :], x[(t - 1) * 128:t * 128, :], hb)
            else:
                K.dma(sp, hb, Bh, hb[:, :], h_d[t * 128:(t + 1) * 128, :], hb)
        return f

    def phase_in_proj(layer):
        with ExitStack() as es:
            alloc_w(es, 4)
            xT = sb(es, "p1_xT", [128, KC, c.TGMAX], BF16)
            rb = rms_bufs(es, "p1")
            ev = [sb(es, f"p1_ev{i}", [128, 512], F32) for i in range(4)]
            evn = 0
            psn = 0
            for g in c.groups:
                TG = len(g) * 128
                rms_to_xT(rb, g, h_src(layer), norm1_g[layer:layer + 1, :], xT, g is c.groups[0])
                jobs = []
                for c0 in range(0, c.INC, SW):
                    def comp(slab, c0=c0, g=g, TG=TG):
                        nonlocal evn, psn
                        if c0 < 2 * c.AW:
                            for j in range(SW // 128):
                                for tb in range(0, TG, 512):
                                    n = min(512, TG - tb)
                                    pb = ps[2 + psn % 6]
                                    psn += 1

                                    def mm(e, pb=pb, j=j, tb=tb, n=n, slab=slab):
                                        ins = None
                                        for kc in range(KC):
                                            ins = e.matmul(pb[:, 0:n], lhsT=slab[:, kc, j * 128:(j + 1) * 128],
                                                           rhs=xT[:, kc, tb:tb + n], start=(kc == 0),
                                                           stop=(kc == KC - 1))
                                        return ins
                                    K.op(pe, [slab, xT], [pb], mm)
                                    e_ = ev[evn % 4]
                                    evn += 1
                                    K.op(act, [pb], [e_], lambda e, e_=e_, pb=pb, n=n: e.copy(out=e_[:, 0:n],
                                                                                             in_=pb[:, 0:n]))
                                    r0 = c0 + j * 128
                                    K.dma(act, BpT, e_, pT_d[r0:r0 + 128, g[0] * 128 + tb:g[0] * 128 + tb + n],
                                          e_[:, 0:n], e_, is_store=True)
                        else:
                            for i, t in enumerate(g):
                                pb = ps[2 + psn % 6]
                                psn += 1

                                def mm(e, pb=pb, i=i, slab=slab):
                                    ins = None
                                    for kc in range(KC):
                                        ins = e.matmul(pb[:, 0:SW], lhsT=xT[:, kc, i * 128:(i + 1) * 128],
                                                       rhs=slab[:, kc, 0:SW], start=(kc == 0), stop=(kc == KC - 1))
                                    return ins
                                K.op(pe, [slab, xT], [pb], mm)
                                e_ = ev[evn % 4]
                                evn += 1
                                K.op(act, [pb], [e_], lambda e, e_=e_, pb=pb: e.copy(out=e_[:, 0:SW], in_=pb[:, 0:SW]))
                                cc = c0 - 2 * c.AW
                                K.dma(act, BpM, e_, pM_d[t * 128:(t + 1) * 128, cc:cc + SW], e_[:, 0:SW], e_,
                                      is_store=True)
                    jobs.append(((w_in[layer], 0, KC, c0, SW), comp))
                run_jobs(jobs, c.groups.index(g))
        K.store_barrier([BpT, BpM])
        K.barrier()

    def phase_out_proj(layer):
        MK = c.MW // 128
        with ExitStack() as es:
            alloc_w(es, 6)
            xT = sb(es, "p4_xT", [128, MK, c.TGMAX], BF16)
            hin = [sb(es, f"p4_hin{i}", [128, SW], F32) for i in range(4)]
            ev = [sb(es, f"p4_ev{i}", [128, SW], F32) for i in range(4)]
            n_ = 0
            for g in c.groups:
                TG = len(g) * 128
                K.dma(sp, xT, BoT, xT[:, :, 0:TG],
                      oT_d[:, g[0] * 128:g[0] * 128 + TG].rearrange("(k p) n -> p k n", p=128), xT)
                jobs = []
                for c0 in range(0, D, SW):
                    def comp(slab, c0=c0, g=g):
                        nonlocal n_
                        hq = act if g is c.groups[0] else pool
                        for i, t in enumerate(g):
                            pb = ps[2 + n_ % 6]
                            hi = hin[n_ % 4]
                            e_ = ev[n_ % 4]
                            n_ += 1
                            hk = hbuf((t, c0))
                            if layer == 0:
                                if t == 0:
                                    K.op(dve, [], [hi], lambda e, hi=hi: e.memset(hi[:, :], 0.0))
                                    K.dma(hq, hi, Bx, hi[0:N_META, :], meta[:, c0:c0 + SW], hi)
                                else:
                                    K.dma(hq, hi, Bx, hi[:, :], x[(t - 1) * 128:t * 128, c0:c0 + SW], hi)
                            else:
                                K.dma(hq, hi, hk, hi[:, :], h_d[t * 128:(t + 1) * 128, c0:c0 + SW], hi)

                            def mm(e, pb=pb, i=i, slab=slab):
                                ins = None
                                for kc in range(MK):
                                    ins = e.matmul(pb[:, 0:SW], lhsT=xT[:, kc, i * 128:(i + 1) * 128],
                                                   rhs=slab[:, kc, 0:SW], start=(kc == 0), stop=(kc == MK - 1))
                                return ins
                            K.op(pe, [slab, xT], [pb], mm)
                            K.op(dve, [pb, hi], [e_], lambda e, e_=e_, pb=pb, hi=hi: e.tensor_tensor(
                                out=e_[:, :], in0=pb[:, 0:SW], in1=hi[:, :], op=ALU.add))
                            K.dma(act, hk, e_, h_d[t * 128:(t + 1) * 128, c0:c0 + SW], e_[:, :], e_, is_store=True)
                    jobs.append(((w_out[layer], 0, MK, c0, SW), comp))
                run_jobs(jobs, c.groups.index(g))
        K.store_barrier([Bh])
        K.barrier()

    def phase_mlp(layer):
        last = (layer == c.DEPTH - 1)
        PC = c.PARTC
        NPART = c.DFF // (PC * 128)
        with ExitStack() as es:
            alloc_w(es)
            xT = sb(es, "p5_xT", [128, KC, c.TGMAX], BF16)
            zT = sb(es, "p5_zT", [128, PC, c.TGMAX], BF16)
            rb = rms_bufs(es, "p5", 1)
            hin = [sb(es, f"p5_hin{i}", [128, SW], F32) for i in range(4)]
            ev = [sb(es, f"p5_ev{i}", [128, SW], F32) for i in range(4)]
            rl = [sb(es, f"p5_rl{i}", [128, 512], F32) for i in range(2)]
            n_ = 0
            m_ = 0
            alias = None
            for g in c.groups:
                TG = len(g) * 128

                def src(t, hb):
                    toks = {}
                    for c0 in range(0, D, SW):
                        K._merge(toks, hbuf((t, c0)).w)
                    tmp = Buf(None, "tmp")
                    tmp.w = toks
                    K.dma(sp, hb, tmp, hb[:, :], h_d[t * 128:(t + 1) * 128, :], hb)
                    for c0 in range(0, D, SW):
                        K._merge(hbuf((t, c0)).r, tmp.r)
                rms_to_xT(rb, g, src, norm2_g[layer:layer + 1, :], xT, g is c.groups[0])
                K.barrier(release=False)
                if alias is None:
                    ht0, ub0 = rb[0][0], rb[1][0]
                    alias = [[], []]
                    if KCMAX * SW * 2 <= D * 4:
                        alias[0].append(Buf(ht0.t[:, 0:KCMAX * SW // 2].bitcast(BF16).rearrange(
                            "p (k n) -> p k n", n=SW), "a_slab"))
                    if 4 * SKC * SW <= D:
                        alias[1].append(Buf(ub0.t[:, 0:2 * SKC * SW].bitcast(F32).rearrange(
                            "p (k n) -> p k n", n=SW), "a_st0"))
                        alias[1].append(Buf(ub0.t[:, 2 * SKC * SW:4 * SKC * SW].bitcast(F32).rearrange(
                            "p (k n) -> p k n", n=SW), "a_st1"))
                wslab.extend(alias[0])
                wstage.extend(alias[1])
                jobs = []
                for part in range(NPART):
                    f0 = part * PC * 128
                    for c0 in range(0, PC * 128, SW):
                        def comp_up(slab, c0=c0, TG=TG):
                            nonlocal m_
                            for j in range(SW // 128):
                                for tb in range(0, TG, 512):
                                    n = min(512, TG - tb)
                                    pb = ps[2 + m_ % 6]
                                    r_ = rl[m_ % 2]
                                    m_ += 1

                                    def mm(e, pb=pb, j=j, tb=tb, n=n, slab=slab):
                                        ins = None
                                        for kc in range(KC):
                                            ins = e.matmul(pb[:, 0:n], lhsT=slab[:, kc, j * 128:(j + 1) * 128],
                                                           rhs=xT[:, kc, tb:tb + n], start=(kc == 0),
                                                           stop=(kc == KC - 1))
                                        return ins
                                    K.op(pe, [slab, xT], [pb], mm)
                                    K.op(act, [pb], [r_], lambda e, r_=r_, pb=pb, n=n: e.activation(
                                        out=r_[:, 0:n], in_=pb[:, 0:n], func=AF.Relu))
                                    zc = (c0 + j * 128) // 128
                                    K.op(dve, [r_], [zT], lambda e, r_=r_, zc=zc, tb=tb, n=n: e.tensor_tensor(
                                        out=zT[:, zc, tb:tb + n], in0=r_[:, 0:n], in1=r_[:, 0:n], op=ALU.mult))
                        jobs.append(((w_up[layer], 0, KC, f0 + c0, SW), comp_up))
                    for c0 in range(0, D, SW):
                        def comp_dn(slab, c0=c0, g=g, part=part):
                            nonlocal m_, n_
                            hq = act if g is c.groups[0] else pool
                            for i, t in enumerate(g):
                                pb = ps[2 + m_ % 6]
                                m_ += 1
                                hi = hin[n_ % 4]
                                e_ = ev[n_ % 4]
                                n_ += 1
                                hk = hbuf((t, c0))
                                K.dma(hq, hi, hk, hi[:, :], h_d[t * 128:(t + 1) * 128, c0:c0 + SW], hi)

                                def mm(e, pb=pb, i=i, slab=slab):
                                    ins = None
                                    for kc in range(PC):
                                        ins = e.matmul(pb[:, 0:SW], lhsT=zT[:, kc, i * 128:(i + 1) * 128],
                                                       rhs=slab[:, kc, 0:SW], start=(kc == 0), stop=(kc == PC - 1))
                                    return ins
                                K.op(pe, [slab, zT], [pb], mm)
                                K.op(dve, [pb, hi], [e_], lambda e, e_=e_, pb=pb, hi=hi: e.tensor_tensor(
                                    out=e_[:, :], in0=pb[:, 0:SW], in1=hi[:, :], op=ALU.add))
                                if last and part == NPART - 1:
                                    if t > 0:
                                        K.dma(act, Bout, e_, out[(t - 1) * 128:t * 128, c0:c0 + SW], e_[:, :], e_,
                                              is_store=True)
                                else:
                                    K.dma(act, hk, e_, h_d[t * 128:(t + 1) * 128, c0:c0 + SW], e_[:, :], e_,
                                          is_store=True)
                        jobs.append(((w_dn[layer], part * PC, PC, c0, SW), comp_dn))
                run_jobs(jobs, c.groups.index(g))
                K.barrier(release=False)
                del wslab[2:]
                del wstage[2:]
        K.store_barrier([Bh, Bout])
        K.barrier()


    def phase_hgrn(layer):
        HA = c.HA
        CB = min(16, NCH)
        with ExitStack() as es:
            F = [sb(es, f"hg_F{i}", [128, LP], F32) for i in range(4)]
            rm = sb(es, "hg_rm", [128, LP], F32)
            qtT = sb(es, "hg_qtT", [128, LP], BF16)
            ktT = sb(es, "hg_ktT", [128, LP], BF16)
            kdT = sb(es, "hg_kdT", [128, LP], BF16)
            oTh = sb(es, "hg_oTh", [128, LP], BF16)
            lbs = sb(es, "hg_lbs", [128, 5 * HA], F32)
            blt = sb(es, "hg_blt", [128, 2 * (NCH + 1)], F32)
            og = sb(es, "hg_og", [64, 128], F32)
            vblk = sb(es, "hg_vblk", [64, CB, 128], F32)
            vB = [sb(es, f"hg_vB{i}", [64, CB, 128], BF16) for i in range(2)]
            gblk = [sb(es, f"hg_g{i}", [64, CB, 128], F32) for i in range(2)]
            oblk = sb(es, "hg_oblk", [64, CB, 128], F32)
            yblk = sb(es, "hg_yblk", [64, CB, 128], BF16)
            kdM = [sb(es, f"hg_kdM{i}", [64, CB, 128], BF16) for i in range(2)]
            am = [sb(es, f"hg_am{i}", [64, 64], BF16) for i in range(2)]
            S = sb(es, "hg_S", [128, 128], F32)
            Sb = [sb(es, f"hg_Sb{i}", [128, 128], BF16) for i in range(2)]
            ssum = sb(es, "hg_ssum", [64, 2 * CB], F32)

            K.dma(sp, rm, Bx, rm[:, :], rmask[0:1, :].partition_broadcast(128), rm)
            K.dma(sp, og, Bx, og[:, :], hgrn_out_g[layer:layer + 1, :].partition_broadcast(64), og)
            nc.sync.dma_start
            ins_src0 = lb_raw[0:1, :].rearrange("o (h d) -> d (o h)", d=128)
            ins_srcl = lb_raw[layer:layer + 1, :].rearrange("o (h d) -> d (o h)", d=128)
            qd = K.sp
            toks = {}
            K._merge(toks, lbs.w)
            K._merge(toks, lbs.r)
            qd.wait_tokens(toks)
            K.get_dsem(lbs)
            for (dst, srcap) in ((lbs[:, 0:HA], ins_src0), (lbs[:, HA:2 * HA], ins_srcl)):
                ins = nc.sync.dma_start(out=dst, in_=srcap, allow_slow_non_contiguous=True)
                lbs.dcnt += 16
                ins.then_inc(lbs.dsem.h, 16)
            lbs.w[lbs.dsem] = lbs.dcnt
            K.all_dma[lbs.dsem] = lbs.dcnt
            if layer == 0:
                K.op(dve, [lbs], [lbs], lambda e: e.memset(lbs[:, 2 * HA:3 * HA], 0.0))
            else:
                K.op(dve, [lbs], [lbs], lambda e: e.tensor_tensor(out=lbs[:, 2 * HA:3 * HA], in0=lbs[:, HA:2 * HA],
                                                                  in1=lbs[:, 0:HA], op=ALU.subtract))
                K.op(act, [lbs], [lbs], lambda e: e.activation(out=lbs[:, 2 * HA:3 * HA], in_=lbs[:, 2 * HA:3 * HA],
                                                               func=AF.Sigmoid))
            K.op(dve, [lbs], [lbs], lambda e: e.tensor_scalar(out=lbs[:, 3 * HA:4 * HA], in0=lbs[:, 2 * HA:3 * HA],
                                                              scalar1=-1.0, scalar2=1.0, op0=ALU.mult, op1=ALU.add))
            K.op(dve, [lbs], [lbs], lambda e: e.tensor_scalar(out=lbs[:, 4 * HA:5 * HA], in0=lbs[:, 2 * HA:3 * HA],
                                                              scalar1=-1.0, scalar2=None, op0=ALU.add))
            pn = {"a": 0, "o": 0, "s": 0, "t": 0}
            for h in range(HA):
                lbh = lbs[:, 2 * HA + h:2 * HA + h + 1]
                omlh = lbs[:, 3 * HA + h:3 * HA + h + 1]
                nomlh = lbs[:, 4 * HA + h:4 * HA + h + 1]
                A_, Bf, Cc, Dd = F
                K.dma(sp, A_, BpT, A_[:, :], pT_d[h * 128:(h + 1) * 128, :], A_)
                K.dma(sp, Bf, BpT, Bf[:, :], pT_d[c.AW + h * 128:c.AW + (h + 1) * 128, :], Bf)
                K.op(pool, [], [oTh], lambda e: e.memset(oTh[:, :], 0.0))
                K.op(act, [Bf], [Bf], lambda e: e.activation(out=Bf[:, :], in_=Bf[:, :], func=AF.Sigmoid))
                K.op(act, [Bf, lbs], [Cc], lambda e: e.activation(out=Cc[:, :], in_=Bf[:, :], func=AF.Ln,
                                                                  bias=lbh, scale=omlh))
                K.op(dve, [Bf, lbs], [Bf], lambda e: e.tensor_scalar(out=Bf[:, :], in0=Bf[:, :], scalar1=nomlh,
                                                                     scalar2=omlh, op0=ALU.mult, op1=ALU.add))
                K.op(dve, [rm, Cc], [Dd], lambda e: e.tensor_tensor_scan(out=Dd[:, :], data0=rm[:, :], data1=Cc[:, :],
                                                                         initial=0.0, op0=ALU.mult, op1=ALU.add))
                K.op(act, [Dd], [Cc], lambda e: e.activation(out=Cc[:, :], in_=Dd[:, :], func=AF.Exp))
                K.op(dve, [A_, Cc], [qtT], lambda e: e.tensor_tensor(out=qtT[:, :], in0=A_[:, :], in1=Cc[:, :],
                                                                     op=ALU.mult))
                K.op(act, [Dd], [Cc], lambda e: e.activation(out=Cc[:, :], in_=Dd[:, :], func=AF.Exp, scale=-1.0))
                K.op(dve, [Bf, Cc], [A_], lambda e: e.tensor_tensor(out=A_[:, :], in0=Bf[:, :], in1=Cc[:, :],
                                                                    op=ALU.mult))
                K.op(pool, [A_], [ktT], lambda e: e.tensor_copy(out=ktT[:, :], in_=A_[:, :]))
                K.op(dve, [Dd], [blt], lambda e: e.tensor_copy(out=blt[:, 0:1], in_=Dd[:, 15:16]))
                K.op(dve, [Dd], [blt], lambda e: e.tensor_copy(
                    out=blt[:, 1:NCH + 1],
                    in_=Dd[:, 128:LP].rearrange("p (c s) -> p c s", s=64)[:, :, 63]))
                K.op(act, [blt], [blt], lambda e: e.activation(out=blt[:, NCH + 1:2 * NCH + 2], in_=blt[:, 0:NCH + 1],
                                                               func=AF.Exp))
                ebl = blt.t[:, NCH + 1:2 * NCH + 2]
                K.op(dve, [A_, blt], [kdT], lambda e: e.tensor_scalar(out=kdT[:, 0:16], in0=A_[:, 0:16],
                                                                      scalar1=ebl[:, 0:1], scalar2=None, op0=ALU.mult))
                K.op(dve, [A_, blt], [kdT], lambda e: e.tensor_tensor(
                    out=kdT[:, 128:LP].rearrange("p (c s) -> p c s", s=64),
                    in0=A_[:, 128:LP].rearrange("p (c s) -> p c s", s=64),
                    in1=ebl[:, 1:NCH + 1].unsqueeze(2).to_broadcast([128, NCH, 64]), op=ALU.mult))
                K.op(dve, [], [S], lambda e: e.memset(S[:, :], 0.0))
                K.op(pool, [], [Sb[0]], lambda e: e.memset(Sb[0][:, :], 0.0))
                cur = 0
                blocks = [(0, [(0, 16, 0)])]
                for b0 in range(0, NCH, CB):
                    blocks.append((1, [(128 + 64 * j, 64, j + 1) for j in range(b0, min(NCH, b0 + CB))]))
                for bi, (isreal, chunks) in enumerate(blocks):
                    nb = len(chunks)
                    C = chunks[0][1]
                    vb_, g_, kd_ = vB[bi % 2], gblk[bi % 2], kdM[bi % 2]
                    r0 = chunks[0][0]
                    icol = 2 * c.AW - 2 * c.AW
                    vsrc = pM_d[r0:r0 + nb * C, h * 128:(h + 1) * 128].rearrange("(c p) e -> p c e", p=C)
                    gsrc = pM_d[r0:r0 + nb * C, c.AW + h * 128:c.AW + (h + 1) * 128].rearrange("(c p) e -> p c e", p=C)
                    K.dma(sp, vblk, BpM, vblk[0:C, 0:nb, :], vsrc, vblk)
                    K.dma(sp, g_, BpM, g_[0:C, 0:nb, :], gsrc, g_)
                    K.op(pool, [vblk], [vb_], lambda e: e.tensor_copy(out=vb_[0:C, 0:nb, :], in_=vblk[0:C, 0:nb, :]))
                    K.op(act, [g_], [g_], lambda e: e.activation(out=g_[0:C, 0:nb, :], in_=g_[0:C, 0:nb, :],
                                                                 func=AF.Silu))
                    K.op(pool, [g_, og], [g_], lambda e: e.tensor_tensor(
                        out=g_[0:C, 0:nb, :], in0=g_[0:C, 0:nb, :],
                        in1=og[0:C, :].unsqueeze(1).to_broadcast([C, nb, 128]), op=ALU.mult))
                    for j8 in range(0, nb, 8):
                        n8 = min(8, nb - j8)
                        pb = ps[6 + pn["t"] % 2]
                        pn["t"] += 1
                        pv = pb.t[:, :].bitcast(BF16)

                        def tr(e, pv=pv, j8=j8, n8=n8):
                            ins = None
                            for j in range(n8):
                                c0 = chunks[j8 + j][0]
                                ins = e.transpose(out=pv[0:C, j * 128:(j + 1) * 128], in_=kdT[:, c0:c0 + C],
                                                  identity=identb[:, :])
                            return ins
                        K.op(pe, [kdT, identb], [pb], tr)
                        K.op(act, [pb], [kd_], lambda e, pv=pv, j8=j8, n8=n8: e.copy(
                            out=kd_[0:C, j8:j8 + n8, :],
                            in_=pv[0:C, 0:n8 * 128].rearrange("p (a b) -> p a b", b=128)))
                    def do_A(j):
                        c0 = chunks[j][0]
                        pa = ps[pn["a"] % 2]
                        pn["a"] += 1
                        am_ = am[j % 2]
                        K.op(pe, [ktT, qtT], [pa], lambda e: e.matmul(
                            pa[0:C, 0:C], lhsT=ktT[:, c0:c0 + C], rhs=qtT[:, c0:c0 + C], start=True, stop=True))
                        K.op(dve, [pa, cst_sb], [am_], lambda e: e.tensor_tensor(
                            out=am_[0:C, 0:C], in0=pa[0:C, 0:C], in1=tri[0:C, 0:C], op=ALU.mult))
                    do_A(0)
                    for j, (c0, C_, jj) in enumerate(chunks):
                        po = ps[2 + pn["o"] % 2]
                        pn["o"] += 1
                        pS = ps[4 + pn["s"] % 2]
                        pn["s"] += 1
                        am_ = am[j % 2]
                        K.op(pe, [kd_, vb_], [pS], lambda e, pS=pS, j=j: e.matmul(
                            pS[:, 0:128], lhsT=kd_[0:C, j, :], rhs=vb_[0:C, j, :], start=True, stop=True))

                        def mmo(e, po=po, am_=am_, j=j, c0=c0, cur=cur):
                            e.matmul(po[0:C, 0:128], lhsT=am_[0:C, 0:C], rhs=vb_[0:C, j, :], start=True, stop=False)
                            return e.matmul(po[0:C, 0:128], lhsT=qtT[:, c0:c0 + C], rhs=Sb[cur][:, :],
                                            start=False, stop=True)
                        K.op(pe, [am_, vb_, qtT, Sb[cur]], [po], mmo)
                        if j + 1 < len(chunks):
                            do_A(j + 1)
                        K.op(act, [po], [oblk], lambda e, po=po, j=j: e.copy(out=oblk[0:C, j, :], in_=po[0:C, 0:128]))
                        K.op(dve, [S, pS, blt], [S], lambda e, pS=pS, jj=jj: e.scalar_tensor_tensor(
                            out=S[:, :], in0=S[:, :], scalar=ebl[:, jj:jj + 1], in1=pS[:, 0:128],
                            op0=ALU.mult, op1=ALU.add))
                        cur = 1 - cur
                        K.op(act, [S], [Sb[cur]], lambda e, cur=cur: e.copy(out=Sb[cur][:, :], in_=S[:, :]))
                    K.op(pool, [oblk], [vblk], lambda e: e.tensor_tensor(
                        out=vblk[0:C, 0:nb, :], in0=oblk[0:C, 0:nb, :], in1=oblk[0:C, 0:nb, :], op=ALU.mult))
                    K.op(dve, [vblk], [ssum], lambda e: e.reduce_sum(out=ssum[0:C, 0:nb], in_=vblk[0:C, 0:nb, :],
                                                                     axis=AX.X))
                    rstd_op(ssum[0:C, CB:CB + nb], ssum[0:C, 0:nb], 128.0, [ssum])
                    K.op(dve, [oblk, ssum], [oblk], lambda e: e.tensor_tensor(
                        out=oblk[0:C, 0:nb, :], in0=oblk[0:C, 0:nb, :],
                        in1=ssum[0:C, CB:CB + nb].unsqueeze(2).to_broadcast([C, nb, 128]), op=ALU.mult))
                    K.op(pool, [oblk, g_], [yblk], lambda e: e.tensor_tensor(
                        out=yblk[0:C, 0:nb, :], in0=oblk[0:C, 0:nb, :], in1=g_[0:C, 0:nb, :], op=ALU.mult))
                    for j8 in range(0, nb, 8):
                        n8 = min(8, nb - j8)
                        pb = ps[6 + pn["t"] % 2]
                        pn["t"] += 1
                        pv = pb.t[:, :].bitcast(BF16)

                        def tr2(e, pv=pv, j8=j8, n8=n8):
                            ins = None
                            for j in range(n8):
                                ins = e.transpose(out=pv[:, j * C:(j + 1) * C], in_=yblk[0:C, j8 + j, :],
                                                  identity=identb[0:C, 0:C])
                            return ins
                        K.op(pe, [yblk, identb], [pb], tr2)
                        c0 = chunks[j8][0]
                        K.op(act, [pb], [oTh], lambda e, pv=pv, c0=c0, n8=n8: e.copy(
                            out=oTh[:, c0:c0 + n8 * C], in_=pv[:, 0:n8 * C]))
                K.dma(act, BoT, oTh, oT_d[h * 128:(h + 1) * 128, :], oTh[:, :], oTh, is_store=True)
        K.store_barrier([BoT])
        K.barrier()


    def phase_attn(layer):
        HB = c.HB
        li = lam_init_of(layer)
        with ExitStack() as es:
            qM = sb(es, "at_qM", [128, NT, 128], F32)
            kM = sb(es, "at_kM", [128, NT, 128], F32)
            vM = sb(es, "at_vM", [128, NT, 128], F32)
            tmp = sb(es, "at_tmp", [128, NT, 128], F32)
            ss = sb(es, "at_ss", [128, 4 * NT], F32)
            gq = sb(es, "at_gq", [128, 128], F32)
            gk = sb(es, "at_gk", [128, 128], F32)
            gdo = sb(es, "at_gdo", [128, 128], F32)
            dl = sb(es, "at_dl", [128, 256], F32)
            lam = sb(es, "at_lam", [128, 8], F32)
            qnM = sb(es, "at_qnM", [128, NT, 128], BF16)
            knM = sb(es, "at_knM", [128, NT, 128], BF16)
            qT = sb(es, "at_qT", [128, LP], BF16)
            kT = sb(es, "at_kT", [128, LP], BF16)
            vB1 = sb(es, "at_vB1", [128, NT, 132], BF16)
            ab = sb(es, "at_ab", [128, 5, 512], F32)
            NB_AT = 6
            stmp = [sb(es, f"at_st{i}", [128, 512], F32) for i in range(NB_AT)]
            P = [sb(es, f"at_P{i}", [128, 512], BF16) for i in range(NB_AT)]
            fo = [sb(es, f"at_fo{i}", [128, 128], F32) for i in range(2)]
            f1 = [sb(es, f"at_f1{i}", [128, 128], F32) for i in range(2)]
            rec = [sb(es, f"at_rec{i}", [128, 4], F32) for i in range(2)]
            yM = sb(es, "at_yM", [128, 4, 128], BF16)
            fin = [sb(es, f"at_fin{i}", [128, 2, 4, 132], F32) for i in range(2)]
            fo4 = [sb(es, f"at_fo4{i}", [128, 4, 128], F32) for i in range(2)]
            f14 = sb(es, "at_f14", [128, 4, 128], F32)
            ss4 = [sb(es, f"at_ss4{i}", [128, 12], F32) for i in range(2)]
            rc4 = [sb(es, f"at_rc4{i}", [128, 2, 4], F32) for i in range(2)]
            oTh = sb(es, "at_oTh", [128, LP], BF16)

            K.dma(sp, gq, Bx, gq[:, :], q_norm_g[layer:layer + 1, :].partition_broadcast(128), gq)
            K.dma(sp, gk, Bx, gk[:, :], k_norm_g[layer:layer + 1, :].partition_broadcast(128), gk)
            K.dma(sp, gdo, Bx, gdo[:, :], diff_out_g[layer:layer + 1, :].partition_broadcast(128), gdo)
            K.dma(sp, dl, Bx, dl[:, :], diff_lambda[layer:layer + 1, :].partition_broadcast(128), dl)
            K.op(dve, [gq], [gq], lambda e: e.tensor_scalar(out=gq[:, :], in0=gq[:, :], scalar1=0.125, scalar2=None,
                                                            op0=ALU.mult))
            K.op(dve, [gdo], [gdo], lambda e: e.tensor_scalar(out=gdo[:, :], in0=gdo[:, :], scalar1=1.0 - li,
                                                              scalar2=None, op0=ALU.mult))
            K.op(dve, [dl], [dl], lambda e: e.tensor_tensor(out=dl[:, 0:64], in0=dl[:, 0:64], in1=dl[:, 64:128],
                                                            op=ALU.mult))
            K.op(dve, [dl], [dl], lambda e: e.tensor_tensor(out=dl[:, 128:192], in0=dl[:, 128:192], in1=dl[:, 192:256],
                                                            op=ALU.mult))
            K.op(dve, [dl], [lam], lambda e: e.reduce_sum(out=lam[:, 0:1], in_=dl[:, 0:64], axis=AX.X))
            K.op(dve, [dl], [lam], lambda e: e.reduce_sum(out=lam[:, 1:2], in_=dl[:, 128:192], axis=AX.X))
            K.op(act, [lam], [lam], lambda e: e.activation(out=lam[:, 2:4], in_=lam[:, 0:2], func=AF.Exp))
            K.op(dve, [lam], [lam], lambda e: e.tensor_tensor(out=lam[:, 4:5], in0=lam[:, 3:4], in1=lam[:, 2:3],
                                                              op=ALU.subtract))
            K.op(dve, [lam], [lam], lambda e: e.tensor_scalar(out=lam[:, 5:6], in0=lam[:, 4:5], scalar1=-li,
                                                              scalar2=None, op0=ALU.add))
            nlam = lam.t[:, 5:6]
            K.op(dve, [], [lam], lambda e: e.memset(lam[:, 6:7], EPS))
            epsb = lam.t[:, 6:7]
            accb = [ps[2], ps[3], ps[4]]

            def acc_of(m, qs):
                i = m * 4 + qs
                return accb[i // 3], (i % 3) * 132
            sn = {"s": 0, "p": 0}
            sbanks = [ps[0], ps[1], ps[5], ps[6]]

            def qknorm(src, gt, dst):
                K.op(pool, [src], [tmp], lambda e: e.tensor_tensor(out=tmp[:, :, :], in0=src[:, :, :], in1=src[:, :, :],
                                                                   op=ALU.mult))
                K.op(dve, [tmp], [ss], lambda e: e.reduce_sum(
                    out=ss[:, 0:2 * NT], in_=tmp[:, :, :].rearrange("p t (c d) -> p (t c) d", d=64), axis=AX.X))
                rstd_op(ss[:, 2 * NT:4 * NT], ss[:, 0:2 * NT], 64.0, [ss])
                K.op(dve, [src, ss], [tmp], lambda e: e.tensor_tensor(
                    out=tmp[:, :, :].rearrange("p t (c d) -> p (t c) d", d=64),
                    in0=src[:, :, :].rearrange("p t (c d) -> p (t c) d", d=64),
                    in1=ss[:, 2 * NT:4 * NT].unsqueeze(2).to_broadcast([128, 2 * NT, 64]), op=ALU.mult))
                K.op(pool, [tmp, gt], [dst], lambda e: e.tensor_tensor(
                    out=dst[:, :, :], in0=tmp[:, :, :], in1=gt[:, :].unsqueeze(1).to_broadcast([128, NT, 128]),
                    op=ALU.mult))

            def to_T(srcM, dstT):
                for t8 in range(0, NT, 4):
                    n8 = min(4, NT - t8)
                    pb = ps[7]
                    pv = pb.t[:, :].bitcast(BF16)

                    def tr(e, t8=t8, n8=n8):
                        ins = None
                        for j in range(n8):
                            ins = e.transpose(out=pv[:, j * 128:(j + 1) * 128], in_=srcM[:, t8 + j, :],
                                              identity=identb[:, :])
                        return ins
                    K.op(pe, [srcM, identb], [pb], tr)
                    K.op(act, [pb], [dstT], lambda e, t8=t8, n8=n8: e.copy(
                        out=dstT[:, t8 * 128:(t8 + n8) * 128], in_=pv[:, 0:n8 * 128]))

            def acc_meta(m, qs):
                return accb[m], 0

            def finalize(rows, qs, slot, acc_of=acc_of):
                fo_, f1_, rec_ = fo[slot % 2], f1[slot % 2], rec[slot % 2]
                b0, o0 = acc_of(0, qs)
                b1, o1 = acc_of(1, qs)
                K.op(dve, [b0], [rec_], lambda e: e.reciprocal(out=rec_[0:rows, 0:1], in_=b0[0:rows, o0 + 128:o0 + 129]))
                K.op(dve, [b1], [rec_], lambda e: e.reciprocal(out=rec_[0:rows, 1:2], in_=b1[0:rows, o1 + 128:o1 + 129]))
                K.op(act, [b0, rec_], [fo_], lambda e: e.activation(out=fo_[0:rows, :], in_=b0[0:rows, o0:o0 + 128],
                                                                     func=AF.Copy, scale=rec_[0:rows, 0:1]))
                K.op(dve, [b1, rec_, lam], [f1_], lambda e: e.tensor_scalar(
                    out=f1_[0:rows, :], in0=b1[0:rows, o1:o1 + 128], scalar1=rec_[0:rows, 1:2],
                    scalar2=nlam[0:rows, :], op0=ALU.mult, op1=ALU.mult))
                K.op(pool, [fo_, f1_], [fo_], lambda e: e.tensor_tensor(out=fo_[0:rows, :], in0=fo_[0:rows, :],
                                                                        in1=f1_[0:rows, :], op=ALU.add))
                K.op(dve, [], [rec_], lambda e: e.memset(rec_[0:rows, 2:3], 0.0))
                K.op(act, [fo_], [f1_, rec_], lambda e: e.activation(out=f1_[0:rows, :], in_=fo_[0:rows, :],
                                                                     func=AF.Square, accum_out=rec_[0:rows, 2:3]))
                rstd_op(rec_[0:rows, 3:4], rec_[0:rows, 2:3], 128.0, [rec_])
                K.op(dve, [fo_, rec_, gdo], [yM], lambda e: e.scalar_tensor_tensor(
                    out=yM[0:rows, qs, :], in0=fo_[0:rows, :], scalar=rec_[0:rows, 3:4], in1=gdo[0:rows, :],
                    op0=ALU.mult, op1=ALU.mult))

            fctr = {"n": 0}

            def fin_a(rows, nq, acc_of_):
                k = fctr["n"] % 2
                fin_, fo_, ss_ = fin[k], fo4[k], ss4[k]
                n_ev = 0
                for m in range(2):
                    for qs in range(nq):
                        bk, off = acc_of_(m, qs)
                        if n_ev % 2 == 0:
                            K.op(act, [bk], [fin_], lambda e: e.copy(out=fin_[0:rows, m, qs, 0:129],
                                                                     in_=bk[0:rows, off:off + 129]))
                        else:
                            K.op(dve, [bk], [fin_], lambda e: e.tensor_copy(out=fin_[0:rows, m, qs, 0:129],
                                                                            in_=bk[0:rows, off:off + 129]))
                        n_ev += 1
                rc_ = rc4[k]
                for m in range(2):
                    K.op(dve, [fin_], [rc_], lambda e: e.reciprocal(out=rc_[0:rows, m, 0:nq],
                                                                    in_=fin_[0:rows, m, 0:nq, 128]))
                K.op(dve, [rc_, lam], [rc_], lambda e: e.tensor_scalar(
                    out=rc_[0:rows, 1, 0:nq], in0=rc_[0:rows, 1, 0:nq], scalar1=nlam[0:rows, :], scalar2=None,
                    op0=ALU.mult))
                K.op(pool, [fin_, rc_], [fo_], lambda e: e.tensor_tensor(
                    out=fo_[0:rows, 0:nq, :], in0=fin_[0:rows, 0, 0:nq, 0:128],
                    in1=rc_[0:rows, 0, 0:nq].unsqueeze(2).to_broadcast([rows, nq, 128]), op=ALU.mult))
                K.op(pool, [fin_, rc_], [f14], lambda e: e.tensor_tensor(
                    out=f14[0:rows, 0:nq, :], in0=fin_[0:rows, 1, 0:nq, 0:128],
                    in1=rc_[0:rows, 1, 0:nq].unsqueeze(2).to_broadcast([rows, nq, 128]), op=ALU.mult))
                K.op(pool, [f14, fo_], [fo_], lambda e: e.tensor_tensor(
                    out=fo_[0:rows, 0:nq, :], in0=f14[0:rows, 0:nq, :], in1=fo_[0:rows, 0:nq, :], op=ALU.add))
                K.op(pool, [fo_], [f14], lambda e: e.tensor_tensor(
                    out=f14[0:rows, 0:nq, :], in0=fo_[0:rows, 0:nq, :], in1=fo_[0:rows, 0:nq, :], op=ALU.mult))
                fctr["n"] += 1
                return k

            def fin_b(rows, nq, k, col0):
                fo_, ss_ = fo4[k], ss4[k]
                K.op(dve, [f14], [ss_], lambda e: e.reduce_sum(out=ss_[0:rows, 0:nq], in_=f14[0:rows, 0:nq, :],
                                                               axis=AX.X))
                K.op(act, [ss_], [ss_], lambda e: e.activation(out=ss_[0:rows, 4:4 + nq], in_=ss_[0:rows, 0:nq],
                                                               func=AF.Ln, scale=1.0 / 128.0, bias=epsb[0:rows, :]))
                K.op(act, [ss_], [ss_], lambda e: e.activation(out=ss_[0:rows, 8:8 + nq], in_=ss_[0:rows, 4:4 + nq],
                                                               func=AF.Exp, scale=-0.5))
                K.op(pool, [fo_, ss_], [fo_], lambda e: e.tensor_tensor(
                    out=fo_[0:rows, 0:nq, :], in0=fo_[0:rows, 0:nq, :],
                    in1=ss_[0:rows, 8:8 + nq].unsqueeze(2).to_broadcast([rows, nq, 128]), op=ALU.mult))
                K.op(pool, [fo_, gdo], [yM], lambda e: e.tensor_tensor(
                    out=yM[0:rows, 0:nq, :], in0=fo_[0:rows, 0:nq, :],
                    in1=gdo[0:rows, :].unsqueeze(1).to_broadcast([rows, nq, 128]), op=ALU.mult))
                y_to_oT(rows, nq, col0)

            def y_to_oT(rows, nq, col0):
                pb = ps[7]
                pv = pb.t[:, :].bitcast(BF16)

                def tr(e):
                    ins = None
                    for j in range(nq):
                        ins = e.transpose(out=pv[:, j * rows:(j + 1) * rows], in_=yM[0:rows, j, :],
                                          identity=identb[0:rows, 0:rows])
                    return ins
                K.op(pe, [yM, identb], [pb], tr)
                K.op(act, [pb], [oTh], lambda e: e.copy(out=oTh[:, col0:col0 + nq * rows], in_=pv[:, 0:nq * rows]))

            for h in range(HB):
                slope = alibi_slope(h, HB)
                cq = 2 * c.AW + h * 128
                K.dma(sp, qM, BpM, qM[:, :, :], pM_d[:, cq:cq + 128].rearrange("(t p) e -> p t e", p=128), qM)
                K.dma(sp, kM, BpM, kM[:, :, :],
                      pM_d[:, cq + c.BW:cq + c.BW + 128].rearrange("(t p) e -> p t e", p=128), kM)
                K.dma(sp, vM, BpM, vM[:, :, :],
                      pM_d[:, cq + 2 * c.BW:cq + 2 * c.BW + 128].rearrange("(t p) e -> p t e", p=128), vM)
                K.dma(sp, ab, Bx, ab[:, :, :], abias[h].rearrange("j k q -> k j q"), ab)
                K.op(pool, [], [oTh], lambda e: e.memset(oTh[:, :], 0.0))
                qknorm(qM, gq, qnM)
                to_T(qnM, qT)
                qknorm(kM, gk, knM)
                to_T(knM, kT)
                K.op(pool, [], [vB1], lambda e: e.memset(vB1[:, :, 128:129], 1.0))
                K.op(pool, [vM], [vB1], lambda e: e.tensor_copy(out=vB1[:, :, 0:128], in_=vM[:, :, :]))

                items = []

                def score_block(kt, nk, nq, qc0, bias_ap, scal, pv_list, acc_of=acc_of, qoff=0):
                    for m in range(2):
                        items.append(("blk", (kt, nk, nq, qc0, bias_ap, scal, pv_list, acc_of, m, qoff)))

                def stage1(it, idx):
                    (kt, nk, nq, qc0, bias_ap, scal, pv_list, acc_of_, m, qoff) = it
                    pb = sbanks[idx % 4]
                    st = stmp[idx % NB_AT]
                    P_ = P[idx % NB_AT]
                    if m == 0:
                        pb1 = sbanks[(idx + 1) % 4]

                        def mm2(e):
                            e.matmul(pb[0:nk, 0:nq], lhsT=kT[0:64, kt * 128:kt * 128 + nk],
                                     rhs=qT[0:64, qc0:qc0 + nq], start=True, stop=True)
                            return e.matmul(pb1[0:nk, 0:nq], lhsT=kT[64:128, kt * 128:kt * 128 + nk],
                                            rhs=qT[64:128, qc0:qc0 + nq], start=True, stop=True)
                        K.op(pe, [kT, qT], [pb, pb1], mm2)
                    K.op(dve, [pb, ab], [st], lambda e: e.tensor_tensor(out=st[0:nk, 0:nq], in0=pb[0:nk, 0:nq],
                                                                        in1=bias_ap, op=ALU.add))
                    K.op(act, [st], [P_], lambda e: e.activation(out=P_[0:nk, 0:nq], in_=st[0:nk, 0:nq],
                                                                 func=AF.Exp, bias=float(scal)))

                def stage2(it, idx):
                    (kt, nk, nq, qc0, bias_ap, scal, pv_list, acc_of_, m, qoff) = it
                    P_ = P[idx % NB_AT]
                    for (qs, first, last) in pv_list:
                        bk, off = acc_of_(m, qs)
                        rows = min(128, nq)
                        K.op(pe, [P_, vB1], [bk], lambda e: e.matmul(
                            bk[0:rows, off:off + 129], lhsT=P_[0:nk, (qs - qoff) * 128:(qs - qoff) * 128 + rows],
                            rhs=vB1[0:nk, kt, 0:129], start=(first and off == 0), stop=last,
                            skip_group_check=True))

                def run_items():
                    LA = 4
                    blk_idx = {}
                    n = 0
                    for p, (kind, it) in enumerate(items):
                        if kind == "blk":
                            blk_idx[p] = n
                            n += 1
                    for p in range(len(items) + LA):
                        if p < len(items) and items[p][0] == "blk":
                            stage1(items[p][1], blk_idx[p])
                        q = p - LA
                        if q >= 0:
                            kind, it = items[q]
                            if kind == "blk":
                                stage2(it, blk_idx[q])
                            else:
                                it()

                score_block(0, 16, 16, 0, ab[0:16, 1, 0:16], 0.0, [(0, True, True)], acc_of=acc_meta)
                def fin_meta():
                    k = fin_a(16, 1, acc_meta)
                    fin_b(16, 1, k, 0)
                items.append(("fin", fin_meta))
                pending_fb = []
                for qb in range(NQB):
                    t0 = 1 + 4 * qb
                    qc0 = t0 * 128
                    for kt in range(0, t0 + 4):
                        if kt == 0:
                            score_block(0, 16, 512, qc0, ab[0:16, 0, :], -slope * (16 + (t0 - 1) * 128),
                                        [(qs, True, False) for qs in range(4)])
                        elif kt < t0:
                            score_block(kt, 128, 512, qc0, ab[:, 0, :], -slope * 128.0 * (t0 - kt),
                                        [(qs, False, False) for qs in range(4)])
                        else:
                            j = kt - t0
                            score_block(kt, 128, 512 - 128 * j, qc0 + 128 * j, ab[:, 1 + j, 128 * j:512], 0.0,
                                        [(qs, False, qs == j) for qs in range(j, 4)], qoff=j)
                    hold = {}

                    def fa(hold=hold):
                        hold["k"] = fin_a(128, 4, acc_of)

                    def fb(hold=hold, qc0=qc0):
                        fin_b(128, 4, hold["k"], qc0)
                    items.append(("fin", fa))
                    pending_fb.append(fb)
                    continue
                new_items = []
                fbq = list(pending_fb)
                nblk_since = None
                for it in items:
                    new_items.append(it)
                    if it[0] == "fin" and it[1].__name__ == "fa":
                        nblk_since = 0
                        cur_fb = fbq.pop(0)
                    elif it[0] == "blk" and nblk_since is not None:
                        nblk_since += 1
                        if nblk_since == 8:
                            new_items.append(("fin", cur_fb))
                            nblk_since = None
                if nblk_since is not None:
                    new_items.append(("fin", cur_fb))
                items[:] = new_items
                run_items()
                K.dma(act, BoT, oTh, oT_d[c.AW + h * 128:c.AW + (h + 1) * 128, :], oTh[:, :], oTh, is_store=True)
        K.store_barrier([BoT])
        K.barrier()

    PH = {"attn": phase_attn, "hgrn": phase_hgrn, "in": phase_in_proj, "out": phase_out_proj, "mlp": phase_mlp}
    if "phase_hgrn" in dir():
        pass
    plist = phases
    if plist is None:
        plist = []
        for layer in range(c.DEPTH):
            plist += [("in", layer), ("hgrn", layer), ("attn", layer), ("out", layer), ("mlp", layer)]
    for name, layer in plist:
        PH[name](layer)
    K.barrier()
    es_top.close()
    return nc


def make_consts(cfg):
    c = cfg
    cst = np.zeros((128, 256), np.float32)
    cst[:, 0:128] = np.eye(128, dtype=np.float32)
    s = np.arange(128)
    cst[:, 128:256] = (s[:, None] <= s[None, :]).astype(np.float32)
    rmask = np.ones((1, c.LP), np.float32)
    rmask[0, 0] = 0
    rmask[0, 16] = 0
    rmask[0, 80] = 0
    rmask[0, 128::64] = 0
    ab = np.zeros((c.HB, 5, 128, 512), np.float32)
    a = np.arange(512)[None, :].astype(np.float64)
    cc = np.arange(128)[:, None].astype(np.float64)
    for h in range(c.HB):
        sl = 2.0 ** (-8.0 * (h + 1) / c.HB)
        ab[h, 0] = -sl * (a - cc)
        for j in range(4):
            kabs = 128 * j + cc
            vis = (kabs // 64) <= (a // 64)
            ab[h, 1 + j] = np.where(vis, -sl * np.abs(a - kabs), -1e30)
    return {"cst": cst, "rmask": rmask, "abias": ab}


def core_inputs(cfg, inputs, b, consts):
    c = cfg
    f = lambda a: np.ascontiguousarray(np.asarray(a, dtype=np.float32))
    m = {
        "x": f(inputs["x"][b]),
        "meta_tokens": f(inputs["meta_tokens"]),
        "norm1_g": f(inputs["norm1_g"]),
        "w_in": f(inputs["w_in"]),
        "hgrn_lb_raw": f(inputs["hgrn_lb_raw"]),
        "hgrn_out_g": f(inputs["hgrn_out_g"]),
        "q_norm_g": f(np.asarray(inputs["q_norm_g"]).reshape(c.DEPTH, 128)),
        "k_norm_g": f(np.asarray(inputs["k_norm_g"]).reshape(c.DEPTH, 128)),
        "diff_lambda": f(np.asarray(inputs["diff_lambda"]).reshape(c.DEPTH, 256)),
        "diff_out_g": f(inputs["diff_out_g"]),
        "w_out": f(inputs["w_out"]),
        "norm2_g": f(inputs["norm2_g"]),
        "w_mlp_up": f(inputs["w_mlp_up"]),
        "w_mlp_down": f(inputs["w_mlp_down"]),
    }
    m.update(consts)
    return m


_CACHE = {}


def kernel(**inputs):
    cfg = Cfg()
    if "nc" not in _CACHE:
        _CACHE["nc"] = build_program(cfg)
        _CACHE["consts"] = make_consts(cfg)
    nc = _CACHE["nc"]
    consts = _CACHE["consts"]
    B = np.asarray(inputs["x"]).shape[0]
    n_cores = 8
    in_maps = [core_inputs(cfg, inputs, i % B, consts) for i in range(n_cores)]
    res = run_bass_kernel_spmd(nc, in_maps, core_ids=list(range(n_cores)))
    outs = [np.asarray(res.results[i]["out"], dtype=np.float32) for i in range(B)]
    return np.stack(outs, axis=0)
```

```python
import math
import numpy as np
import concourse.bass as bass
import concourse.mybir as mybir
from concourse.bass_utils import run_bass_kernel_spmd

F32 = mybir.dt.float32
BF16 = mybir.dt.bfloat16
AF = mybir.ActivationFunctionType
ALU = mybir.AluOpType
AX = mybir.AxisListType

EPS = 1e-6
N_META = 16


class Cfg:
    def __init__(self, D=4096, SEQ=4096, HA=16, HB=16, DFF=16384, DEPTH=2, GROUP=8, PARTC=16):
        self.D, self.SEQ, self.HA, self.HB, self.DFF, self.DEPTH = D, SEQ, HA, HB, DFF, DEPTH
        self.KC = D // 128
        self.NT = 1 + SEQ // 128
        self.LP = self.NT * 128
        self.AW = HA * 128
        self.BW = HB * 128
        self.MW = self.AW + self.BW
        self.INC = 4 * self.AW + 3 * self.BW
        self.PMW = 2 * self.AW + 3 * self.BW
        self.GROUP = GROUP
        self.PARTC = min(PARTC, DFF // 128)
        gs = []
        t = 0
        first = True
        while t < self.NT:
            n = GROUP + 1 if first else GROUP
            gs.append(list(range(t, min(self.NT, t + n))))
            t += n
            first = False
        self.groups = gs
        self.TGMAX = max(len(g) for g in gs) * 128


class Sem:
    __slots__ = ("h", "name")

    def __init__(self, h, name):
        self.h = h
        self.name = name


class Buf:
    __slots__ = ("t", "w", "r", "dsem", "dcnt", "name")

    def __init__(self, t, name):
        self.t = t
        self.name = name
        self.w = {}
        self.r = {}
        self.dsem = None
        self.dcnt = 0

    def __getitem__(self, k):
        return self.t[k]


class Eng:
    def __init__(self, K, name, h):
        self.K = K
        self.name = name
        self.h = h
        self.sem = K.new_sem("e_" + name)
        self.cnt = 0
        self.seen = {}

    def wait_tokens(self, toks):
        for s, v in toks.items():
            if s is self.sem and self.name == "pe":
                continue
            if self.seen.get(s, 0) >= v:
                continue
            self.h.wait_ge(s.h, v)
            self.seen[s] = v


class Kern:
    def __init__(self, nc):
        self.nc = nc
        self.nsem = 0
        self.pe = Eng(self, "pe", nc.tensor)
        self.act = Eng(self, "act", nc.scalar)
        self.dve = Eng(self, "dve", nc.vector)
        self.pool = Eng(self, "pool", nc.gpsimd)
        self.sp = Eng(self, "sp", nc.sync)
        self.engs = [self.pe, self.act, self.dve, self.pool, self.sp]
        self.store_toks = {}
        self.all_dma = {}
        self.free_sems = []
        self.phase_bufs = []

    def new_sem(self, name):
        self.nsem += 1
        return Sem(self.nc.alloc_semaphore(name=f"{name}_{self.nsem}"), name)

    @staticmethod
    def _merge(d, s):
        for k, v in s.items():
            if d.get(k, 0) < v:
                d[k] = v

    def op(self, eng, reads, writes, fn):
        toks = {}
        for b in reads:
            self._merge(toks, b.w)
        for b in writes:
            self._merge(toks, b.w)
            self._merge(toks, b.r)
        eng.wait_tokens(toks)
        ins = fn(eng.h)
        eng.cnt += 1
        ins.then_inc(eng.sem.h, 1)
        for b in reads:
            b.r[eng.sem] = eng.cnt
        for b in writes:
            b.w[eng.sem] = eng.cnt
        return ins

    def dma(self, q, dst, src, dst_ap, src_ap, sb, is_store=False):
        toks = {}
        self._merge(toks, src.w)
        self._merge(toks, dst.w)
        self._merge(toks, dst.r)
        q.wait_tokens(toks)
        ins = q.h.dma_start(out=dst_ap, in_=src_ap)
        self.get_dsem(sb)
        sb.dcnt += 16
        ins.then_inc(sb.dsem.h, 16)
        src.r[sb.dsem] = sb.dcnt
        dst.w[sb.dsem] = sb.dcnt
        self.all_dma[sb.dsem] = sb.dcnt
        if is_store:
            self.store_toks[sb.dsem] = sb.dcnt
        return ins

    def get_dsem(self, sb):
        if sb.dsem is None:
            if self.free_sems:
                sb.dsem, sb.dcnt = self.free_sems.pop()
            else:
                sb.dsem = self.new_sem("d")
            self.phase_bufs.append(sb)

    def barrier(self, release=True):
        toks = {}
        for e in self.engs:
            if e.cnt:
                toks[e.sem] = e.cnt
        self._merge(toks, self.all_dma)
        for e in self.engs:
            e.wait_tokens(toks)
        if release:
            for b in self.phase_bufs:
                self.free_sems.append((b.dsem, b.dcnt))
            self.phase_bufs = []

    def store_barrier(self, dram_bufs):
        for b in dram_bufs:
            self._merge(b.w, self.store_toks)


from contextlib import ExitStack


def lam_init_of(layer):
    return 0.8 - 0.6 * math.exp(-0.3 * layer)


def alibi_slope(h, n):
    return float(2.0 ** (-8.0 * (h + 1) / n))


def build_program(cfg, debug=None, phases=None):
    c = cfg
    nc = bass.Bass("TRN2", target_bir_lowering=False)
    D, KC, NT, LP = c.D, c.KC, c.NT, c.LP
    NCH = c.SEQ // 64
    NQB = c.SEQ // 512
    debug = debug or []

    def din(name, shape, dtype=F32):
        return nc.dram_tensor(name, list(shape), dtype, kind="ExternalInput").ap()

    x = din("x", [c.SEQ, D])
    meta = din("meta_tokens", [N_META, D])
    norm1_g = din("norm1_g", [c.DEPTH, D])
    w_in = din("w_in", [c.DEPTH, D, c.INC])
    lb_raw = din("hgrn_lb_raw", [c.DEPTH, c.AW])
    hgrn_out_g = din("hgrn_out_g", [c.DEPTH, 128])
    q_norm_g = din("q_norm_g", [c.DEPTH, 128])
    k_norm_g = din("k_norm_g", [c.DEPTH, 128])
    diff_lambda = din("diff_lambda", [c.DEPTH, 256])
    diff_out_g = din("diff_out_g", [c.DEPTH, 128])
    w_out = din("w_out", [c.DEPTH, c.MW, D])
    norm2_g = din("norm2_g", [c.DEPTH, D])
    w_up = din("w_mlp_up", [c.DEPTH, D, c.DFF])
    w_dn = din("w_mlp_down", [c.DEPTH, c.DFF, D])
    cst = din("cst", [128, 256])
    rmask = din("rmask", [1, LP])
    abias = din("abias", [c.HB, 5, 128, 512])
    out = nc.dram_tensor("out", [c.SEQ, D], F32, kind="ExternalOutput").ap()

    def dscr(name, shape, dtype):
        kind = "ExternalOutput" if name in debug else "Internal"
        return nc.dram_tensor(name, list(shape), dtype, kind=kind).ap()

    h_d = dscr("h_scr", [LP, D], F32)
    pT_d = dscr("pT_scr", [2 * c.AW, LP], F32)
    pM_d = dscr("pM_scr", [LP, c.PMW], F32)
    oT_d = dscr("oT_scr", [c.MW, LP], BF16)

    K = Kern(nc)
    pe, act, dve, pool, sp = K.pe, K.act, K.dve, K.pool, K.sp
    es_top = ExitStack()

    nmc = {"n": 0}

    def sb(es, name, shape, dtype):
        nmc["n"] += 1
        name = f"{name}_{nmc['n']}"
        t = es.enter_context(nc.sbuf_tensor(name, list(shape), dtype))
        return Buf(t, name)

    Bx = Buf(None, "x")
    Bh = Buf(None, "h")
    BpT = Buf(None, "pT")
    BpM = Buf(None, "pM")
    BoT = Buf(None, "oT")
    Bout = Buf(None, "out")
    hpiece = {}

    def hbuf(key):
        if key not in hpiece:
            hpiece[key] = Buf(None, "hp")
        return hpiece[key]

    ps = [Buf(es_top.enter_context(nc.psum_tensor(f"ps{i}", [128, 512], F32)), f"ps{i}") for i in range(8)]

    cst_sb = sb(es_top, "cst_sb", [128, 256], F32)
    K.dma(sp, cst_sb, Bx, cst_sb[:, :], cst[:, :], cst_sb)
    identb = sb(es_top, "identb", [128, 128], BF16)
    K.op(dve, [cst_sb], [identb], lambda e: e.tensor_copy(out=identb[:, :], in_=cst_sb[:, 0:128]))
    ident = cst_sb.t[:, 0:128]
    tri = cst_sb.t[:, 128:256]

    SW = 256
    KCMAX = max(KC, c.MW // 128, c.PARTC)
    SKC = 4
    wslab = []
    wstage = []
    wctr = {"slab": 0, "stage": 0}

    def alloc_w(es, nstage=2):
        wslab[:] = [sb(es, f"wslab{i}", [128, KCMAX, SW], BF16) for i in range(2)]
        wstage[:] = [sb(es, f"wstage{i}", [128, SKC, SW], F32) for i in range(nstage)]

    NSID = max(c.INC // SW, D // SW, (c.PARTC * 128) // SW * (c.DFF // (c.PARTC * 128)) + (D // SW) * (c.DFF // (c.PARTC * 128)))
    wbf_l = [nc.dram_tensor(f"wbf_scr{i}", [64, 128, KCMAX * SW], BF16, kind="Internal").ap()
             for i in range((NSID + 63) // 64)]

    class _WB:
        def __getitem__(self, k):
            sid = k[0]
            return wbf_l[sid // 64][(sid % 64,) + tuple(k[1:])]
    wbf_d = _WB()
    wbuf = {}

    def wb(sid):
        if sid not in wbuf:
            wbuf[sid] = Buf(None, "wb")
        return wbuf[sid]

    def load_slab(la, gi, sid):
        wap, k0, nk, c0, ncols = la
        slab = wslab[wctr["slab"] % len(wslab)]
        wctr["slab"] += 1
        if gi > 0:
            K.dma(sp, slab, wb(sid), slab[:, 0:nk, :],
                  wbf_d[sid, :, 0:nk * SW].rearrange("p (k n) -> p k n", n=SW), slab)
            return slab
        for kk in range(0, nk, SKC):
            n = min(SKC, nk - kk)
            st = wstage[wctr["stage"] % len(wstage)]
            wctr["stage"] += 1
            src = wap[(k0 + kk) * 128:(k0 + kk + n) * 128, c0:c0 + ncols].rearrange("(k p) n -> p k n", p=128)
            K.dma(sp, st, Bx, st[:, 0:n, 0:ncols], src, st)
            K.op(pool, [st], [slab], lambda e, st=st, slab=slab, kk=kk, n=n:
                 e.tensor_copy(out=slab[:, kk:kk + n, 0:ncols], in_=st[:, 0:n, 0:ncols]))
        if len(c.groups) > 1:
            K.dma(pool, wb(sid), slab, wbf_d[sid, :, 0:nk * SW].rearrange("p (k n) -> p k n", n=SW),
                  slab[:, 0:nk, :], slab, is_store=True)
        return slab

    def run_jobs(jobs, gi):
        if gi == 0:
            for sid, (la, fn) in enumerate(jobs):
                fn(load_slab(la, 0, sid))
            return
        cur = load_slab(jobs[0][0], gi, 0)
        for i, (la, fn) in enumerate(jobs):
            nxt = load_slab(jobs[i + 1][0], gi, i + 1) if i + 1 < len(jobs) else None
            fn(cur)
            cur = nxt

    evc = {"n": 0}

    def rstd_op(o, i, n, bufs):
        K.op(dve, bufs, bufs, lambda e: e.tensor_scalar(out=o, in0=i, scalar1=1.0 / n, scalar2=EPS,
                                                        op0=ALU.mult, op1=ALU.add))
        K.op(act, bufs, bufs, lambda e: e.sqrt(out=o, in_=o))
        K.op(dve, bufs, bufs, lambda e: e.reciprocal(out=o, in_=o))

    def rms_bufs(es, name, nb=2):
        ht = [sb(es, f"{name}_ht{i}", [128, D], F32) for i in range(nb)]
        ub = [sb(es, f"{name}_ub{i}", [128, D], BF16) for i in range(nb)]
        st = [sb(es, f"{name}_st{i}", [128, 2], F32) for i in range(nb)]
        gt = sb(es, f"{name}_gt", [128, D], F32)
        return ht, ub, st, gt

    def rms_to_xT(rb, tiles, src_of_tile, g_ap, xT, first):
        ht, ub, st, gt = rb
        if first:
            K.dma(sp, gt, Bx, gt[:, :], g_ap.partition_broadcast(128), gt)
        for i, t in enumerate(tiles):
            hb, u, s = ht[i % len(ht)], ub[i % len(ht)], st[i % len(ht)]
            src_of_tile(t, hb)
            K.op(dve, [], [s], lambda e: e.memset(s[:, 0:1], 0.0))
            K.op(act, [hb], [u, s], lambda e: e.activation(out=u[:, :], in_=hb[:, :], func=AF.Square,
                                                           accum_out=s[:, 0:1]))
            rstd_op(s[:, 1:2], s[:, 0:1], float(D), [s])
            K.op(dve, [hb, s, gt], [u], lambda e: e.scalar_tensor_tensor(
                out=u[:, :], in0=hb[:, :], scalar=s[:, 1:2], in1=gt[:, :], op0=ALU.mult, op1=ALU.mult))
            for k8 in range(0, KC, 8):
                n = min(8, KC - k8)
                pb = ps[evc["n"] % 2]
                evc["n"] += 1
                pv = pb.t[:, :].bitcast(BF16)

                def tr(e, u=u, pv=pv, k8=k8, n=n):
                    ins = None
                    for j in range(n):
                        ins = e.transpose(out=pv[:, j * 128:(j + 1) * 128],
                                          in_=u[:, (k8 + j) * 128:(k8 + j + 1) * 128], identity=identb[:, :])
                    return ins
                K.op(pe, [u, identb], [pb], tr)
                K.op(act, [pb], [xT], lambda e, pv=pv, k8=k8, n=n, i=i: e.copy(
                    out=xT[:, k8:k8 + n, i * 128:(i + 1) * 128],
                    in_=pv[:, 0:n * 128].rearrange("p (a b) -> p a b", b=128)))

    def h_src(layer):
        def f(t, hb):
            if layer == 0:
                if t == 0:
                    K.op(pool, [], [hb], lambda e: e.memset(hb[:, :], 0.0))
                    K.dma(sp, hb, Bx, hb[0:N_META, :], meta[:, :], hb)
                else:
                    K.dma(sp, hb, Bx, hb[:, :], x[(t - 1) * 128:t * 128, :], hb)
            else:
                K.dma(sp, hb, Bh, hb[:, :], h_d[t * 128:(t + 1) * 128, :], hb)
        return f

    def phase_in_proj(layer):
        with ExitStack() as es:
            alloc_w(es, 4)
            xT = sb(es, "p1_xT", [128, KC, c.TGMAX], BF16)
            rb = rms_bufs(es, "p1")
            ev = [sb(es, f"p1_ev{i}", [128, 512], F32) for i in range(4)]
            evn = 0
            psn = 0
            for g in c.groups:
                TG = len(g) * 128
                rms_to_xT(rb, g, h_src(layer), norm1_g[layer:layer + 1, :], xT, g is c.groups[0])
                jobs = []
                for c0 in range(0, c.INC, SW):
                    def comp(slab, c0=c0, g=g, TG=TG):
                        nonlocal evn, psn
                        if c0 < 2 * c.AW:
                            for j in range(SW // 128):
                                for tb in range(0, TG, 512):
                                    n = min(512, TG - tb)
                                    pb = ps[2 + psn % 6]
                                    psn += 1

                                    def mm(e, pb=pb, j=j, tb=tb, n=n, slab=slab):
                                        ins = None
                                        for kc in range(KC):
                                            ins = e.matmul(pb[:, 0:n], lhsT=slab[:, kc, j * 128:(j + 1) * 128],
                                                           rhs=xT[:, kc, tb:tb + n], start=(kc == 0),
                                                           stop=(kc == KC - 1))
                                        return ins
                                    K.op(pe, [slab, xT], [pb], mm)
                                    e_ = ev[evn % 4]
                                    evn += 1
                                    K.op(act, [pb], [e_], lambda e, e_=e_, pb=pb, n=n: e.copy(out=e_[:, 0:n],
                                                                                             in_=pb[:, 0:n]))
                                    r0 = c0 + j * 128
                                    K.dma(act, BpT, e_, pT_d[r0:r0 + 128, g[0] * 128 + tb:g[0] * 128 + tb + n],
                                          e_[:, 0:n], e_, is_store=True)
                        else:
                            for i, t in enumerate(g):
                                pb = ps[2 + psn % 6]
                                psn += 1

                                def mm(e, pb=pb, i=i, slab=slab):
                                    ins = None
                                    for kc in range(KC):
                                        ins = e.matmul(pb[:, 0:SW], lhsT=xT[:, kc, i * 128:(i + 1) * 128],
                                                       rhs=slab[:, kc, 0:SW], start=(kc == 0), stop=(kc == KC - 1))
                                    return ins
                                K.op(pe, [slab, xT], [pb], mm)
                                e_ = ev[evn % 4]
                                evn += 1
                                K.op(act, [pb], [e_], lambda e, e_=e_, pb=pb: e.copy(out=e_[:, 0:SW], in_=pb[:, 0:SW]))
                                cc = c0 - 2 * c.AW
                                K.dma(act, BpM, e_, pM_d[t * 128:(t + 1) * 128, cc:cc + SW], e_[:, 0:SW], e_,
                                      is_store=True)
                    jobs.append(((w_in[layer], 0, KC, c0, SW), comp))
                run_jobs(jobs, c.groups.index(g))
        K.store_barrier([BpT, BpM])
        K.barrier()

    def phase_out_proj(layer):
        MK = c.MW // 128
        with ExitStack() as es:
            alloc_w(es, 6)
            xT = sb(es, "p4_xT", [128, MK, c.TGMAX], BF16)
            hin = [sb(es, f"p4_hin{i}", [128, SW], F32) for i in range(4)]
            ev = [sb(es, f"p4_ev{i}", [128, SW], F32) for i in range(4)]
            n_ = 0
            for g in c.groups:
                TG = len(g) * 128
                K.dma(sp, xT, BoT, xT[:, :, 0:TG],
                      oT_d[:, g[0] * 128:g[0] * 128 + TG].rearrange("(k p) n -> p k n", p=128), xT)
                jobs = []
                for c0 in range(0, D, SW):
                    def comp(slab, c0=c0, g=g):
                        nonlocal n_
                        hq = act if g is c.groups[0] else pool
                        for i, t in enumerate(g):
                            pb = ps[2 + n_ % 6]
                            hi = hin[n_ % 4]
                            e_ = ev[n_ % 4]
                            n_ += 1
                            hk = hbuf((t, c0))
                            if layer == 0:
                                if t == 0:
                                    K.op(dve, [], [hi], lambda e, hi=hi: e.memset(hi[:, :], 0.0))
                                    K.dma(hq, hi, Bx, hi[0:N_META, :], meta[:, c0:c0 + SW], hi)
                                else:
                                    K.dma(hq, hi, Bx, hi[:, :], x[(t - 1) * 128:t * 128, c0:c0 + SW], hi)
                            else:
                                K.dma(hq, hi, hk, hi[:, :], h_d[t * 128:(t + 1) * 128, c0:c0 + SW], hi)

                            def mm(e, pb=pb, i=i, slab=slab):
                                ins = None
                                for kc in range(MK):
                                    ins = e.matmul(pb[:, 0:SW], lhsT=xT[:, kc, i * 128:(i + 1) * 128],
                                                   rhs=slab[:, kc, 0:SW], start=(kc == 0), stop=(kc == MK - 1))
                                return ins
                            K.op(pe, [slab, xT], [pb], mm)
                            K.op(dve, [pb, hi], [e_], lambda e, e_=e_, pb=pb, hi=hi: e.tensor_tensor(
                                out=e_[:, :], in0=pb[:, 0:SW], in1=hi[:, :], op=ALU.add))
                            K.dma(act, hk, e_, h_d[t * 128:(t + 1) * 128, c0:c0 + SW], e_[:, :], e_, is_store=True)
                    jobs.append(((w_out[layer], 0, MK, c0, SW), comp))
                run_jobs(jobs, c.groups.index(g))
        K.store_barrier([Bh])
        K.barrier()

    def phase_mlp(layer):
        last = (layer == c.DEPTH - 1)
        PC = c.PARTC
        NPART = c.DFF // (PC * 128)
        with ExitStack() as es:
            alloc_w(es)
            xT = sb(es, "p5_xT", [128, KC, c.TGMAX], BF16)
            zT = sb(es, "p5_zT", [128, PC, c.TGMAX], BF16)
            rb = rms_bufs(es, "p5", 1)
            hin = [sb(es, f"p5_hin{i}", [128, SW], F32) for i in range(4)]
            ev = [sb(es, f"p5_ev{i}", [128, SW], F32) for i in range(4)]
            rl = [sb(es, f"p5_rl{i}", [128, 512], F32) for i in range(2)]
            n_ = 0
            m_ = 0
            alias = None
            for g in c.groups:
                TG = len(g) * 128

                def src(t, hb):
                    toks = {}
                    for c0 in range(0, D, SW):
                        K._merge(toks, hbuf((t, c0)).w)
                    tmp = Buf(None, "tmp")
                    tmp.w = toks
                    K.dma(sp, hb, tmp, hb[:, :], h_d[t * 128:(t + 1) * 128, :], hb)
                    for c0 in range(0, D, SW):
                        K._merge(hbuf((t, c0)).r, tmp.r)
                rms_to_xT(rb, g, src, norm2_g[layer:layer + 1, :], xT, g is c.groups[0])
                K.barrier(release=False)
                if alias is None:
                    ht0, ub0 = rb[0][0], rb[1][0]
                    alias = [[], []]
                    if KCMAX * SW * 2 <= D * 4:
                        alias[0].append(Buf(ht0.t[:, 0:KCMAX * SW // 2].bitcast(BF16).rearrange(
                            "p (k n) -> p k n", n=SW), "a_slab"))
                    if 4 * SKC * SW <= D:
                        alias[1].append(Buf(ub0.t[:, 0:2 * SKC * SW].bitcast(F32).rearrange(
                            "p (k n) -> p k n", n=SW), "a_st0"))
                        alias[1].append(Buf(ub0.t[:, 2 * SKC * SW:4 * SKC * SW].bitcast(F32).rearrange(
                            "p (k n) -> p k n", n=SW), "a_st1"))
                wslab.extend(alias[0])
                wstage.extend(alias[1])
                jobs = []
                for part in range(NPART):
                    f0 = part * PC * 128
                    for c0 in range(0, PC * 128, SW):
                        def comp_up(slab, c0=c0, TG=TG):
                            nonlocal m_
                            for j in range(SW // 128):
                                for tb in range(0, TG, 512):
                                    n = min(512, TG - tb)
                                    pb = ps[2 + m_ % 6]
                                    r_ = rl[m_ % 2]
                                    m_ += 1

                                    def mm(e, pb=pb, j=j, tb=tb, n=n, slab=slab):
                                        ins = None
                                        for kc in range(KC):
                                            ins = e.matmul(pb[:, 0:n], lhsT=slab[:, kc, j * 128:(j + 1) * 128],
                                                           rhs=xT[:, kc, tb:tb + n], start=(kc == 0),
                                                           stop=(kc == KC - 1))
                                        return ins
                                    K.op(pe, [slab, xT], [pb], mm)
                                    K.op(act, [pb], [r_], lambda e, r_=r_, pb=pb, n=n: e.activation(
                                        out=r_[:, 0:n], in_=pb[:, 0:n], func=AF.Relu))
                                    zc = (c0 + j * 128) // 128
                                    K.op(dve, [r_], [zT], lambda e, r_=r_, zc=zc, tb=tb, n=n: e.tensor_tensor(
                                        out=zT[:, zc, tb:tb + n], in0=r_[:, 0:n], in1=r_[:, 0:n], op=ALU.mult))
                        jobs.append(((w_up[layer], 0, KC, f0 + c0, SW), comp_up))
                    for c0 in range(0, D, SW):
                        def comp_dn(slab, c0=c0, g=g, part=part):
                            nonlocal m_, n_
                            hq = act if g is c.groups[0] else pool

                            def ld(t_, k_):
                                K.dma(hq, hin[k_ % 4], hbuf((t_, c0)), hin[k_ % 4][:, :],
                                      h_d[t_ * 128:(t_ + 1) * 128, c0:c0 + SW], hin[k_ % 4])
                            ld(g[0], n_)
                            for i, t in enumerate(g):
                                pb = ps[2 + m_ % 6]
                                m_ += 1
                                hi = hin[n_ % 4]
                                e_ = ev[n_ % 4]
                                n_ += 1
                                hk = hbuf((t, c0))
                                if i + 1 < len(g):
                                    ld(g[i + 1], n_)

                                def mm(e, pb=pb, i=i, slab=slab):
                                    ins = None
                                    for kc in range(PC):
                                        ins = e.matmul(pb[:, 0:SW], lhsT=zT[:, kc, i * 128:(i + 1) * 128],
                                                       rhs=slab[:, kc, 0:SW], start=(kc == 0), stop=(kc == PC - 1))
                                    return ins
                                K.op(pe, [slab, zT], [pb], mm)
                                K.op(dve, [pb, hi], [e_], lambda e, e_=e_, pb=pb, hi=hi: e.tensor_tensor(
                                    out=e_[:, :], in0=pb[:, 0:SW], in1=hi[:, :], op=ALU.add))
                                if last and part == NPART - 1:
                                    if t > 0:
                                        K.dma(act, Bout, e_, out[(t - 1) * 128:t * 128, c0:c0 + SW], e_[:, :], e_,
                                              is_store=True)
                                else:
                                    K.dma(act, hk, e_, h_d[t * 128:(t + 1) * 128, c0:c0 + SW], e_[:, :], e_,
                                          is_store=True)
                        jobs.append(((w_dn[layer], part * PC, PC, c0, SW), comp_dn))
                run_jobs(jobs, c.groups.index(g))
                K.barrier(release=False)
                del wslab[2:]
                del wstage[2:]
        K.store_barrier([Bh, Bout])
        K.barrier()


    def phase_hgrn(layer):
        HA = c.HA
        CB = min(16, NCH)
        with ExitStack() as es:
            F = [sb(es, f"hg_F{i}", [128, LP], F32) for i in range(4)]
            rm = sb(es, "hg_rm", [128, LP], F32)
            qtT = sb(es, "hg_qtT", [128, LP], BF16)
            ktT = sb(es, "hg_ktT", [128, LP], BF16)
            kdT = sb(es, "hg_kdT", [128, LP], BF16)
            oTh = sb(es, "hg_oTh", [128, LP], BF16)
            lbs = sb(es, "hg_lbs", [128, 5 * HA], F32)
            blt = sb(es, "hg_blt", [128, 2 * (NCH + 1)], F32)
            og = sb(es, "hg_og", [64, 128], F32)
            vblk = sb(es, "hg_vblk", [64, CB, 128], F32)
            vB = [sb(es, f"hg_vB{i}", [64, CB, 128], BF16) for i in range(2)]
            gblk = [sb(es, f"hg_g{i}", [64, CB, 128], F32) for i in range(2)]
            oblk = sb(es, "hg_oblk", [64, CB, 128], F32)
            yblk = sb(es, "hg_yblk", [64, CB, 128], BF16)
            kdM = [sb(es, f"hg_kdM{i}", [64, CB, 128], BF16) for i in range(2)]
            am = [sb(es, f"hg_am{i}", [64, 64], BF16) for i in range(2)]
            S = sb(es, "hg_S", [128, 128], F32)
            Sb = [sb(es, f"hg_Sb{i}", [128, 128], BF16) for i in range(2)]
            ssum = sb(es, "hg_ssum", [64, 2 * CB], F32)

            K.dma(sp, rm, Bx, rm[:, :], rmask[0:1, :].partition_broadcast(128), rm)
            K.dma(sp, og, Bx, og[:, :], hgrn_out_g[layer:layer + 1, :].partition_broadcast(64), og)
            nc.sync.dma_start
            ins_src0 = lb_raw[0:1, :].rearrange("o (h d) -> d (o h)", d=128)
            ins_srcl = lb_raw[layer:layer + 1, :].rearrange("o (h d) -> d (o h)", d=128)
            qd = K.sp
            toks = {}
            K._merge(toks, lbs.w)
            K._merge(toks, lbs.r)
            qd.wait_tokens(toks)
            K.get_dsem(lbs)
            for (dst, srcap) in ((lbs[:, 0:HA], ins_src0), (lbs[:, HA:2 * HA], ins_srcl)):
                ins = nc.sync.dma_start(out=dst, in_=srcap, allow_slow_non_contiguous=True)
                lbs.dcnt += 16
                ins.then_inc(lbs.dsem.h, 16)
            lbs.w[lbs.dsem] = lbs.dcnt
            K.all_dma[lbs.dsem] = lbs.dcnt
            if layer == 0:
                K.op(dve, [lbs], [lbs], lambda e: e.memset(lbs[:, 2 * HA:3 * HA], 0.0))
            else:
                K.op(dve, [lbs], [lbs], lambda e: e.tensor_tensor(out=lbs[:, 2 * HA:3 * HA], in0=lbs[:, HA:2 * HA],
                                                                  in1=lbs[:, 0:HA], op=ALU.subtract))
                K.op(act, [lbs], [lbs], lambda e: e.activation(out=lbs[:, 2 * HA:3 * HA], in_=lbs[:, 2 * HA:3 * HA],
                                                               func=AF.Sigmoid))
            K.op(dve, [lbs], [lbs], lambda e: e.tensor_scalar(out=lbs[:, 3 * HA:4 * HA], in0=lbs[:, 2 * HA:3 * HA],
                                                              scalar1=-1.0, scalar2=1.0, op0=ALU.mult, op1=ALU.add))
            K.op(dve, [lbs], [lbs], lambda e: e.tensor_scalar(out=lbs[:, 4 * HA:5 * HA], in0=lbs[:, 2 * HA:3 * HA],
                                                              scalar1=-1.0, scalar2=None, op0=ALU.add))
            pn = {"a": 0, "o": 0, "s": 0, "t": 0}
            for h in range(HA):
                lbh = lbs[:, 2 * HA + h:2 * HA + h + 1]
                omlh = lbs[:, 3 * HA + h:3 * HA + h + 1]
                nomlh = lbs[:, 4 * HA + h:4 * HA + h + 1]
                A_, Bf, Cc, Dd = F
                K.dma(sp, A_, BpT, A_[:, :], pT_d[h * 128:(h + 1) * 128, :], A_)
                K.dma(sp, Bf, BpT, Bf[:, :], pT_d[c.AW + h * 128:c.AW + (h + 1) * 128, :], Bf)
                K.op(pool, [], [oTh], lambda e: e.memset(oTh[:, :], 0.0))
                K.op(act, [Bf], [Bf], lambda e: e.activation(out=Bf[:, :], in_=Bf[:, :], func=AF.Sigmoid))
                K.op(act, [Bf, lbs], [Cc], lambda e: e.activation(out=Cc[:, :], in_=Bf[:, :], func=AF.Ln,
                                                                  bias=lbh, scale=omlh))
                K.op(dve, [Bf, lbs], [Bf], lambda e: e.tensor_scalar(out=Bf[:, :], in0=Bf[:, :], scalar1=nomlh,
                                                                     scalar2=omlh, op0=ALU.mult, op1=ALU.add))
                K.op(dve, [rm, Cc], [Dd], lambda e: e.tensor_tensor_scan(out=Dd[:, :], data0=rm[:, :], data1=Cc[:, :],
                                                                         initial=0.0, op0=ALU.mult, op1=ALU.add))
                K.op(act, [Dd], [Cc], lambda e: e.activation(out=Cc[:, :], in_=Dd[:, :], func=AF.Exp))
                K.op(dve, [A_, Cc], [qtT], lambda e: e.tensor_tensor(out=qtT[:, :], in0=A_[:, :], in1=Cc[:, :],
                                                                     op=ALU.mult))
                K.op(act, [Dd], [Cc], lambda e: e.activation(out=Cc[:, :], in_=Dd[:, :], func=AF.Exp, scale=-1.0))
                K.op(dve, [Bf, Cc], [A_], lambda e: e.tensor_tensor(out=A_[:, :], in0=Bf[:, :], in1=Cc[:, :],
                                                                    op=ALU.mult))
                K.op(pool, [A_], [ktT], lambda e: e.tensor_copy(out=ktT[:, :], in_=A_[:, :]))
                K.op(dve, [Dd], [blt], lambda e: e.tensor_copy(out=blt[:, 0:1], in_=Dd[:, 15:16]))
                K.op(dve, [Dd], [blt], lambda e: e.tensor_copy(
                    out=blt[:, 1:NCH + 1],
                    in_=Dd[:, 128:LP].rearrange("p (c s) -> p c s", s=64)[:, :, 63]))
                K.op(act, [blt], [blt], lambda e: e.activation(out=blt[:, NCH + 1:2 * NCH + 2], in_=blt[:, 0:NCH + 1],
                                                               func=AF.Exp))
                ebl = blt.t[:, NCH + 1:2 * NCH + 2]
                K.op(dve, [A_, blt], [kdT], lambda e: e.tensor_scalar(out=kdT[:, 0:16], in0=A_[:, 0:16],
                                                                      scalar1=ebl[:, 0:1], scalar2=None, op0=ALU.mult))
                K.op(dve, [A_, blt], [kdT], lambda e: e.tensor_tensor(
                    out=kdT[:, 128:LP].rearrange("p (c s) -> p c s", s=64),
                    in0=A_[:, 128:LP].rearrange("p (c s) -> p c s", s=64),
                    in1=ebl[:, 1:NCH + 1].unsqueeze(2).to_broadcast([128, NCH, 64]), op=ALU.mult))
                K.op(dve, [], [S], lambda e: e.memset(S[:, :], 0.0))
                K.op(pool, [], [Sb[0]], lambda e: e.memset(Sb[0][:, :], 0.0))
                cur = 0
                blocks = [(0, [(0, 16, 0)])]
                for b0 in range(0, NCH, CB):
                    blocks.append((1, [(128 + 64 * j, 64, j + 1) for j in range(b0, min(NCH, b0 + CB))]))
                for bi, (isreal, chunks) in enumerate(blocks):
                    nb = len(chunks)
                    C = chunks[0][1]
                    vb_, g_, kd_ = vB[bi % 2], gblk[bi % 2], kdM[bi % 2]
                    r0 = chunks[0][0]
                    icol = 2 * c.AW - 2 * c.AW
                    vsrc = pM_d[r0:r0 + nb * C, h * 128:(h + 1) * 128].rearrange("(c p) e -> p c e", p=C)
                    gsrc = pM_d[r0:r0 + nb * C, c.AW + h * 128:c.AW + (h + 1) * 128].rearrange("(c p) e -> p c e", p=C)
                    K.dma(sp, vblk, BpM, vblk[0:C, 0:nb, :], vsrc, vblk)
                    K.dma(sp, g_, BpM, g_[0:C, 0:nb, :], gsrc, g_)
                    K.op(pool, [vblk], [vb_], lambda e: e.tensor_copy(out=vb_[0:C, 0:nb, :], in_=vblk[0:C, 0:nb, :]))
                    K.op(act, [g_], [g_], lambda e: e.activation(out=g_[0:C, 0:nb, :], in_=g_[0:C, 0:nb, :],
                                                                 func=AF.Silu))
                    K.op(pool, [g_, og], [g_], lambda e: e.tensor_tensor(
                        out=g_[0:C, 0:nb, :], in0=g_[0:C, 0:nb, :],
                        in1=og[0:C, :].unsqueeze(1).to_broadcast([C, nb, 128]), op=ALU.mult))
                    for j8 in range(0, nb, 8):
                        n8 = min(8, nb - j8)
                        pb = ps[6 + pn["t"] % 2]
                        pn["t"] += 1
                        pv = pb.t[:, :].bitcast(BF16)

                        def tr(e, pv=pv, j8=j8, n8=n8):
                            ins = None
                            for j in range(n8):
                                c0 = chunks[j8 + j][0]
                                ins = e.transpose(out=pv[0:C, j * 128:(j + 1) * 128], in_=kdT[:, c0:c0 + C],
                                                  identity=identb[:, :])
                            return ins
                        K.op(pe, [kdT, identb], [pb], tr)
                        K.op(act, [pb], [kd_], lambda e, pv=pv, j8=j8, n8=n8: e.copy(
                            out=kd_[0:C, j8:j8 + n8, :],
                            in_=pv[0:C, 0:n8 * 128].rearrange("p (a b) -> p a b", b=128)))
                    def do_A(j):
                        c0 = chunks[j][0]
                        pa = ps[pn["a"] % 2]
                        pn["a"] += 1
                        am_ = am[j % 2]
                        K.op(pe, [ktT, qtT], [pa], lambda e: e.matmul(
                            pa[0:C, 0:C], lhsT=ktT[:, c0:c0 + C], rhs=qtT[:, c0:c0 + C], start=True, stop=True))
                        K.op(dve, [pa, cst_sb], [am_], lambda e: e.tensor_tensor(
                            out=am_[0:C, 0:C], in0=pa[0:C, 0:C], in1=tri[0:C, 0:C], op=ALU.mult))
                    do_A(0)
                    for j, (c0, C_, jj) in enumerate(chunks):
                        po = ps[2 + pn["o"] % 2]
                        pn["o"] += 1
                        pS = ps[4 + pn["s"] % 2]
                        pn["s"] += 1
                        am_ = am[j % 2]
                        K.op(pe, [kd_, vb_], [pS], lambda e, pS=pS, j=j: e.matmul(
                            pS[:, 0:128], lhsT=kd_[0:C, j, :], rhs=vb_[0:C, j, :], start=True, stop=True))

                        def mmo(e, po=po, am_=am_, j=j, c0=c0, cur=cur):
                            e.matmul(po[0:C, 0:128], lhsT=am_[0:C, 0:C], rhs=vb_[0:C, j, :], start=True, stop=False)
                            return e.matmul(po[0:C, 0:128], lhsT=qtT[:, c0:c0 + C], rhs=Sb[cur][:, :],
                                            start=False, stop=True)
                        K.op(pe, [am_, vb_, qtT, Sb[cur]], [po], mmo)
                        if j + 1 < len(chunks):
                            do_A(j + 1)
                        K.op(act, [po], [oblk], lambda e, po=po, j=j: e.copy(out=oblk[0:C, j, :], in_=po[0:C, 0:128]))
                        K.op(dve, [S, pS, blt], [S], lambda e, pS=pS, jj=jj: e.scalar_tensor_tensor(
                            out=S[:, :], in0=S[:, :], scalar=ebl[:, jj:jj + 1], in1=pS[:, 0:128],
                            op0=ALU.mult, op1=ALU.add))
                        cur = 1 - cur
                        K.op(act, [S], [Sb[cur]], lambda e, cur=cur: e.copy(out=Sb[cur][:, :], in_=S[:, :]))
                    K.op(pool, [oblk], [vblk], lambda e: e.tensor_tensor(
                        out=vblk[0:C, 0:nb, :], in0=oblk[0:C, 0:nb, :], in1=oblk[0:C, 0:nb, :], op=ALU.mult))
                    K.op(dve, [vblk], [ssum], lambda e: e.reduce_sum(out=ssum[0:C, 0:nb], in_=vblk[0:C, 0:nb, :],
                                                                     axis=AX.X))
                    rstd_op(ssum[0:C, CB:CB + nb], ssum[0:C, 0:nb], 128.0, [ssum])
                    K.op(dve, [oblk, ssum], [oblk], lambda e: e.tensor_tensor(
                        out=oblk[0:C, 0:nb, :], in0=oblk[0:C, 0:nb, :],
                        in1=ssum[0:C, CB:CB + nb].unsqueeze(2).to_broadcast([C, nb, 128]), op=ALU.mult))
                    K.op(pool, [oblk, g_], [yblk], lambda e: e.tensor_tensor(
                        out=yblk[0:C, 0:nb, :], in0=oblk[0:C, 0:nb, :], in1=g_[0:C, 0:nb, :], op=ALU.mult))
                    for j8 in range(0, nb, 8):
                        n8 = min(8, nb - j8)
                        pb = ps[6 + pn["t"] % 2]
                        pn["t"] += 1
                        pv = pb.t[:, :].bitcast(BF16)

                        def tr2(e, pv=pv, j8=j8, n8=n8):
                            ins = None
                            for j in range(n8):
                                ins = e.transpose(out=pv[:, j * C:(j + 1) * C], in_=yblk[0:C, j8 + j, :],
                                                  identity=identb[0:C, 0:C])
                            return ins
                        K.op(pe, [yblk, identb], [pb], tr2)
                        c0 = chunks[j8][0]
                        K.op(act, [pb], [oTh], lambda e, pv=pv, c0=c0, n8=n8: e.copy(
                            out=oTh[:, c0:c0 + n8 * C], in_=pv[:, 0:n8 * C]))
                K.dma(act, BoT, oTh, oT_d[h * 128:(h + 1) * 128, :], oTh[:, :], oTh, is_store=True)
        K.store_barrier([BoT])
        K.barrier()


    def phase_attn(layer):
        HB = c.HB
        li = lam_init_of(layer)
        with ExitStack() as es:
            qM = sb(es, "at_qM", [128, NT, 128], F32)
            kM = sb(es, "at_kM", [128, NT, 128], F32)
            vM = sb(es, "at_vM", [128, NT, 128], F32)
            tmp = sb(es, "at_tmp", [128, NT, 128], F32)
            ss = sb(es, "at_ss", [128, 4 * NT], F32)
            gq = sb(es, "at_gq", [128, 128], F32)
            gk = sb(es, "at_gk", [128, 128], F32)
            gdo = sb(es, "at_gdo", [128, 128], F32)
            dl = sb(es, "at_dl", [128, 256], F32)
            lam = sb(es, "at_lam", [128, 8], F32)
            qnM = sb(es, "at_qnM", [128, NT, 128], BF16)
            knM = sb(es, "at_knM", [128, NT, 128], BF16)
            qT = sb(es, "at_qT", [128, LP], BF16)
            kT = sb(es, "at_kT", [128, LP], BF16)
            vB1 = sb(es, "at_vB1", [128, NT, 132], BF16)
            ab = sb(es, "at_ab", [128, 5, 512], F32)
            NB_AT = 6
            stmp = [sb(es, f"at_st{i}", [128, 512], F32) for i in range(NB_AT)]
            P = [sb(es, f"at_P{i}", [128, 512], BF16) for i in range(NB_AT)]
            fo = [sb(es, f"at_fo{i}", [128, 128], F32) for i in range(2)]
            f1 = [sb(es, f"at_f1{i}", [128, 128], F32) for i in range(2)]
            rec = [sb(es, f"at_rec{i}", [128, 4], F32) for i in range(2)]
            yM = sb(es, "at_yM", [128, 4, 128], BF16)
            fin = [sb(es, f"at_fin{i}", [128, 2, 4, 132], F32) for i in range(2)]
            fo4 = [sb(es, f"at_fo4{i}", [128, 4, 128], F32) for i in range(2)]
            f14 = sb(es, "at_f14", [128, 4, 128], F32)
            ss4 = [sb(es, f"at_ss4{i}", [128, 12], F32) for i in range(2)]
            rc4 = [sb(es, f"at_rc4{i}", [128, 2, 4], F32) for i in range(2)]
            oTh = sb(es, "at_oTh", [128, LP], BF16)

            K.dma(sp, gq, Bx, gq[:, :], q_norm_g[layer:layer + 1, :].partition_broadcast(128), gq)
            K.dma(sp, gk, Bx, gk[:, :], k_norm_g[layer:layer + 1, :].partition_broadcast(128), gk)
            K.dma(sp, gdo, Bx, gdo[:, :], diff_out_g[layer:layer + 1, :].partition_broadcast(128), gdo)
            K.dma(sp, dl, Bx, dl[:, :], diff_lambda[layer:layer + 1, :].partition_broadcast(128), dl)
            K.op(dve, [gq], [gq], lambda e: e.tensor_scalar(out=gq[:, :], in0=gq[:, :], scalar1=0.125, scalar2=None,
                                                            op0=ALU.mult))
            K.op(dve, [gdo], [gdo], lambda e: e.tensor_scalar(out=gdo[:, :], in0=gdo[:, :], scalar1=1.0 - li,
                                                              scalar2=None, op0=ALU.mult))
            K.op(dve, [dl], [dl], lambda e: e.tensor_tensor(out=dl[:, 0:64], in0=dl[:, 0:64], in1=dl[:, 64:128],
                                                            op=ALU.mult))
            K.op(dve, [dl], [dl], lambda e: e.tensor_tensor(out=dl[:, 128:192], in0=dl[:, 128:192], in1=dl[:, 192:256],
                                                            op=ALU.mult))
            K.op(dve, [dl], [lam], lambda e: e.reduce_sum(out=lam[:, 0:1], in_=dl[:, 0:64], axis=AX.X))
            K.op(dve, [dl], [lam], lambda e: e.reduce_sum(out=lam[:, 1:2], in_=dl[:, 128:192], axis=AX.X))
            K.op(act, [lam], [lam], lambda e: e.activation(out=lam[:, 2:4], in_=lam[:, 0:2], func=AF.Exp))
            K.op(dve, [lam], [lam], lambda e: e.tensor_tensor(out=lam[:, 4:5], in0=lam[:, 3:4], in1=lam[:, 2:3],
                                                              op=ALU.subtract))
            K.op(dve, [lam], [lam], lambda e: e.tensor_scalar(out=lam[:, 5:6], in0=lam[:, 4:5], scalar1=-li,
                                                              scalar2=None, op0=ALU.add))
            nlam = lam.t[:, 5:6]
            K.op(dve, [], [lam], lambda e: e.memset(lam[:, 6:7], EPS))
            epsb = lam.t[:, 6:7]
            accb = [ps[2], ps[3], ps[4]]

            def acc_of(m, qs):
                i = m * 4 + qs
                return accb[i // 3], (i % 3) * 132
            sn = {"s": 0, "p": 0}
            sbanks = [ps[0], ps[1], ps[5], ps[6]]

            def qknorm(src, gt, dst):
                K.op(pool, [src], [tmp], lambda e: e.tensor_tensor(out=tmp[:, :, :], in0=src[:, :, :], in1=src[:, :, :],
                                                                   op=ALU.mult))
                K.op(dve, [tmp], [ss], lambda e: e.reduce_sum(
                    out=ss[:, 0:2 * NT], in_=tmp[:, :, :].rearrange("p t (c d) -> p (t c) d", d=64), axis=AX.X))
                rstd_op(ss[:, 2 * NT:4 * NT], ss[:, 0:2 * NT], 64.0, [ss])
                K.op(dve, [src, ss], [tmp], lambda e: e.tensor_tensor(
                    out=tmp[:, :, :].rearrange("p t (c d) -> p (t c) d", d=64),
                    in0=src[:, :, :].rearrange("p t (c d) -> p (t c) d", d=64),
                    in1=ss[:, 2 * NT:4 * NT].unsqueeze(2).to_broadcast([128, 2 * NT, 64]), op=ALU.mult))
                K.op(pool, [tmp, gt], [dst], lambda e: e.tensor_tensor(
                    out=dst[:, :, :], in0=tmp[:, :, :], in1=gt[:, :].unsqueeze(1).to_broadcast([128, NT, 128]),
                    op=ALU.mult))

            def to_T(srcM, dstT):
                for t8 in range(0, NT, 4):
                    n8 = min(4, NT - t8)
                    pb = ps[7]
                    pv = pb.t[:, :].bitcast(BF16)

                    def tr(e, t8=t8, n8=n8):
                        ins = None
                        for j in range(n8):
                            ins = e.transpose(out=pv[:, j * 128:(j + 1) * 128], in_=srcM[:, t8 + j, :],
                                              identity=identb[:, :])
                        return ins
                    K.op(pe, [srcM, identb], [pb], tr)
                    K.op(act, [pb], [dstT], lambda e, t8=t8, n8=n8: e.copy(
                        out=dstT[:, t8 * 128:(t8 + n8) * 128], in_=pv[:, 0:n8 * 128]))

            def acc_meta(m, qs):
                return accb[m], 0

            def finalize(rows, qs, slot, acc_of=acc_of):
                fo_, f1_, rec_ = fo[slot % 2], f1[slot % 2], rec[slot % 2]
                b0, o0 = acc_of(0, qs)
                b1, o1 = acc_of(1, qs)
                K.op(dve, [b0], [rec_], lambda e: e.reciprocal(out=rec_[0:rows, 0:1], in_=b0[0:rows, o0 + 128:o0 + 129]))
                K.op(dve, [b1], [rec_], lambda e: e.reciprocal(out=rec_[0:rows, 1:2], in_=b1[0:rows, o1 + 128:o1 + 129]))
                K.op(act, [b0, rec_], [fo_], lambda e: e.activation(out=fo_[0:rows, :], in_=b0[0:rows, o0:o0 + 128],
                                                                     func=AF.Copy, scale=rec_[0:rows, 0:1]))
                K.op(dve, [b1, rec_, lam], [f1_], lambda e: e.tensor_scalar(
                    out=f1_[0:rows, :], in0=b1[0:rows, o1:o1 + 128], scalar1=rec_[0:rows, 1:2],
                    scalar2=nlam[0:rows, :], op0=ALU.mult, op1=ALU.mult))
                K.op(pool, [fo_, f1_], [fo_], lambda e: e.tensor_tensor(out=fo_[0:rows, :], in0=fo_[0:rows, :],
                                                                        in1=f1_[0:rows, :], op=ALU.add))
                K.op(dve, [], [rec_], lambda e: e.memset(rec_[0:rows, 2:3], 0.0))
                K.op(act, [fo_], [f1_, rec_], lambda e: e.activation(out=f1_[0:rows, :], in_=fo_[0:rows, :],
                                                                     func=AF.Square, accum_out=rec_[0:rows, 2:3]))
                rstd_op(rec_[0:rows, 3:4], rec_[0:rows, 2:3], 128.0, [rec_])
                K.op(dve, [fo_, rec_, gdo], [yM], lambda e: e.scalar_tensor_tensor(
                    out=yM[0:rows, qs, :], in0=fo_[0:rows, :], scalar=rec_[0:rows, 3:4], in1=gdo[0:rows, :],
                    op0=ALU.mult, op1=ALU.mult))

            fctr = {"n": 0}

            def fin_a(rows, nq, acc_of_):
                k = fctr["n"] % 2
                fin_, fo_, ss_ = fin[k], fo4[k], ss4[k]
                n_ev = 0
                for m in range(2):
                    for qs in range(nq):
                        bk, off = acc_of_(m, qs)
                        if n_ev % 2 == 0:
                            K.op(act, [bk], [fin_], lambda e: e.copy(out=fin_[0:rows, m, qs, 0:129],
                                                                     in_=bk[0:rows, off:off + 129]))
                        else:
                            K.op(dve, [bk], [fin_], lambda e: e.tensor_copy(out=fin_[0:rows, m, qs, 0:129],
                                                                            in_=bk[0:rows, off:off + 129]))
                        n_ev += 1
                rc_ = rc4[k]
                for m in range(2):
                    K.op(dve, [fin_], [rc_], lambda e: e.reciprocal(out=rc_[0:rows, m, 0:nq],
                                                                    in_=fin_[0:rows, m, 0:nq, 128]))
                K.op(dve, [rc_, lam], [rc_], lambda e: e.tensor_scalar(
                    out=rc_[0:rows, 1, 0:nq], in0=rc_[0:rows, 1, 0:nq], scalar1=nlam[0:rows, :], scalar2=None,
                    op0=ALU.mult))
                K.op(pool, [fin_, rc_], [fo_], lambda e: e.tensor_tensor(
                    out=fo_[0:rows, 0:nq, :], in0=fin_[0:rows, 0, 0:nq, 0:128],
                    in1=rc_[0:rows, 0, 0:nq].unsqueeze(2).to_broadcast([rows, nq, 128]), op=ALU.mult))
                K.op(pool, [fin_, rc_], [f14], lambda e: e.tensor_tensor(
                    out=f14[0:rows, 0:nq, :], in0=fin_[0:rows, 1, 0:nq, 0:128],
                    in1=rc_[0:rows, 1, 0:nq].unsqueeze(2).to_broadcast([rows, nq, 128]), op=ALU.mult))
                K.op(pool, [f14, fo_], [fo_], lambda e: e.tensor_tensor(
                    out=fo_[0:rows, 0:nq, :], in0=f14[0:rows, 0:nq, :], in1=fo_[0:rows, 0:nq, :], op=ALU.add))
                K.op(pool, [fo_], [f14], lambda e: e.tensor_tensor(
                    out=f14[0:rows, 0:nq, :], in0=fo_[0:rows, 0:nq, :], in1=fo_[0:rows, 0:nq, :], op=ALU.mult))
                fctr["n"] += 1
                return k

            def fin_b(rows, nq, k, col0):
                fo_, ss_ = fo4[k], ss4[k]
                K.op(dve, [f14], [ss_], lambda e: e.reduce_sum(out=ss_[0:rows, 0:nq], in_=f14[0:rows, 0:nq, :],
                                                               axis=AX.X))
                K.op(act, [ss_], [ss_], lambda e: e.activation(out=ss_[0:rows, 4:4 + nq], in_=ss_[0:rows, 0:nq],
                                                               func=AF.Ln, scale=1.0 / 128.0, bias=epsb[0:rows, :]))
                K.op(act, [ss_], [ss_], lambda e: e.activation(out=ss_[0:rows, 8:8 + nq], in_=ss_[0:rows, 4:4 + nq],
                                                               func=AF.Exp, scale=-0.5))
                K.op(pool, [fo_, ss_], [fo_], lambda e: e.tensor_tensor(
                    out=fo_[0:rows, 0:nq, :], in0=fo_[0:rows, 0:nq, :],
                    in1=ss_[0:rows, 8:8 + nq].unsqueeze(2).to_broadcast([rows, nq, 128]), op=ALU.mult))
                K.op(pool, [fo_, gdo], [yM], lambda e: e.tensor_tensor(
                    out=yM[0:rows, 0:nq, :], in0=fo_[0:rows, 0:nq, :],
                    in1=gdo[0:rows, :].unsqueeze(1).to_broadcast([rows, nq, 128]), op=ALU.mult))
                y_to_oT(rows, nq, col0)

            def y_to_oT(rows, nq, col0):
                pb = ps[7]
                pv = pb.t[:, :].bitcast(BF16)

                def tr(e):
                    ins = None
                    for j in range(nq):
                        ins = e.transpose(out=pv[:, j * rows:(j + 1) * rows], in_=yM[0:rows, j, :],
                                          identity=identb[0:rows, 0:rows])
                    return ins
                K.op(pe, [yM, identb], [pb], tr)
                K.op(act, [pb], [oTh], lambda e: e.copy(out=oTh[:, col0:col0 + nq * rows], in_=pv[:, 0:nq * rows]))

            for h in range(HB):
                slope = alibi_slope(h, HB)
                cq = 2 * c.AW + h * 128
                K.dma(sp, qM, BpM, qM[:, :, :], pM_d[:, cq:cq + 128].rearrange("(t p) e -> p t e", p=128), qM)
                K.dma(sp, kM, BpM, kM[:, :, :],
                      pM_d[:, cq + c.BW:cq + c.BW + 128].rearrange("(t p) e -> p t e", p=128), kM)
                K.dma(sp, vM, BpM, vM[:, :, :],
                      pM_d[:, cq + 2 * c.BW:cq + 2 * c.BW + 128].rearrange("(t p) e -> p t e", p=128), vM)
                K.dma(sp, ab, Bx, ab[:, :, :], abias[h].rearrange("j k q -> k j q"), ab)
                K.op(pool, [], [oTh], lambda e: e.memset(oTh[:, :], 0.0))
                qknorm(qM, gq, qnM)
                to_T(qnM, qT)
                qknorm(kM, gk, knM)
                to_T(knM, kT)
                K.op(pool, [], [vB1], lambda e: e.memset(vB1[:, :, 128:129], 1.0))
                K.op(pool, [vM], [vB1], lambda e: e.tensor_copy(out=vB1[:, :, 0:128], in_=vM[:, :, :]))

                items = []

                def score_block(kt, nk, nq, qc0, bias_ap, scal, pv_list, acc_of=acc_of, qoff=0):
                    for m in range(2):
                        items.append(("blk", (kt, nk, nq, qc0, bias_ap, scal, pv_list, acc_of, m, qoff)))

                def stage1(it, idx):
                    (kt, nk, nq, qc0, bias_ap, scal, pv_list, acc_of_, m, qoff) = it
                    pb = sbanks[idx % 4]
                    st = stmp[idx % NB_AT]
                    P_ = P[idx % NB_AT]
                    if m == 0:
                        pb1 = sbanks[(idx + 1) % 4]

                        def mm2(e):
                            e.matmul(pb[0:nk, 0:nq], lhsT=kT[0:64, kt * 128:kt * 128 + nk],
                                     rhs=qT[0:64, qc0:qc0 + nq], start=True, stop=True)
                            return e.matmul(pb1[0:nk, 0:nq], lhsT=kT[64:128, kt * 128:kt * 128 + nk],
                                            rhs=qT[64:128, qc0:qc0 + nq], start=True, stop=True)
                        K.op(pe, [kT, qT], [pb, pb1], mm2)
                    K.op(dve, [pb, ab], [st], lambda e: e.tensor_tensor(out=st[0:nk, 0:nq], in0=pb[0:nk, 0:nq],
                                                                        in1=bias_ap, op=ALU.add))
                    K.op(act, [st], [P_], lambda e: e.activation(out=P_[0:nk, 0:nq], in_=st[0:nk, 0:nq],
                                                                 func=AF.Exp, bias=float(scal)))

                def stage2(it, idx):
                    (kt, nk, nq, qc0, bias_ap, scal, pv_list, acc_of_, m, qoff) = it
                    P_ = P[idx % NB_AT]
                    for (qs, first, last) in pv_list:
                        bk, off = acc_of_(m, qs)
                        rows = min(128, nq)
                        K.op(pe, [P_, vB1], [bk], lambda e: e.matmul(
                            bk[0:rows, off:off + 129], lhsT=P_[0:nk, (qs - qoff) * 128:(qs - qoff) * 128 + rows],
                            rhs=vB1[0:nk, kt, 0:129], start=(first and off == 0), stop=last,
                            skip_group_check=True))

                def run_items():
                    LA = 4
                    blk_idx = {}
                    n = 0
                    for p, (kind, it) in enumerate(items):
                        if kind == "blk":
                            blk_idx[p] = n
                            n += 1
                    for p in range(len(items) + LA):
                        if p < len(items) and items[p][0] == "blk":
                            stage1(items[p][1], blk_idx[p])
                        q = p - LA
                        if q >= 0:
                            kind, it = items[q]
                            if kind == "blk":
                                stage2(it, blk_idx[q])
                            else:
                                it()

                score_block(0, 16, 16, 0, ab[0:16, 1, 0:16], 0.0, [(0, True, True)], acc_of=acc_meta)
                def fin_meta():
                    k = fin_a(16, 1, acc_meta)
                    fin_b(16, 1, k, 0)
                items.append(("fin", fin_meta))
                pending_fb = []
                for qb in range(NQB):
                    t0 = 1 + 4 * qb
                    qc0 = t0 * 128
                    for kt in range(0, t0 + 4):
                        if kt == 0:
                            score_block(0, 16, 512, qc0, ab[0:16, 0, :], -slope * (16 + (t0 - 1) * 128),
                                        [(qs, True, False) for qs in range(4)])
                        elif kt < t0:
                            score_block(kt, 128, 512, qc0, ab[:, 0, :], -slope * 128.0 * (t0 - kt),
                                        [(qs, False, False) for qs in range(4)])
                        else:
                            j = kt - t0
                            score_block(kt, 128, 512 - 128 * j, qc0 + 128 * j, ab[:, 1 + j, 128 * j:512], 0.0,
                                        [(qs, False, qs == j) for qs in range(j, 4)], qoff=j)
                    hold = {}

                    def fa(hold=hold):
                        hold["k"] = fin_a(128, 4, acc_of)

                    def fb(hold=hold, qc0=qc0):
                        fin_b(128, 4, hold["k"], qc0)
                    items.append(("fin", fa))
                    pending_fb.append(fb)
                    continue
                new_items = []
                fbq = list(pending_fb)
                nblk_since = None
                for it in items:
                    new_items.append(it)
                    if it[0] == "fin" and it[1].__name__ == "fa":
                        nblk_since = 0
                        cur_fb = fbq.pop(0)
                    elif it[0] == "blk" and nblk_since is not None:
                        nblk_since += 1
                        if nblk_since == 8:
                            new_items.append(("fin", cur_fb))
                            nblk_since = None
                if nblk_since is not None:
                    new_items.append(("fin", cur_fb))
                items[:] = new_items
                run_items()
                K.dma(act, BoT, oTh, oT_d[c.AW + h * 128:c.AW + (h + 1) * 128, :], oTh[:, :], oTh, is_store=True)
        K.store_barrier([BoT])
        K.barrier()

    PH = {"attn": phase_attn, "hgrn": phase_hgrn, "in": phase_in_proj, "out": phase_out_proj, "mlp": phase_mlp}
    if "phase_hgrn" in dir():
        pass
    plist = phases
    if plist is None:
        plist = []
        for layer in range(c.DEPTH):
            plist += [("in", layer), ("hgrn", layer), ("attn", layer), ("out", layer), ("mlp", layer)]
    for name, layer in plist:
        PH[name](layer)
    K.barrier()
    es_top.close()
    return nc


def make_consts(cfg):
    c = cfg
    cst = np.zeros((128, 256), np.float32)
    cst[:, 0:128] = np.eye(128, dtype=np.float32)
    s = np.arange(128)
    cst[:, 128:256] = (s[:, None] <= s[None, :]).astype(np.float32)
    rmask = np.ones((1, c.LP), np.float32)
    rmask[0, 0] = 0
    rmask[0, 16] = 0
    rmask[0, 80] = 0
    rmask[0, 128::64] = 0
    ab = np.zeros((c.HB, 5, 128, 512), np.float32)
    a = np.arange(512)[None, :].astype(np.float64)
    cc = np.arange(128)[:, None].astype(np.float64)
    for h in range(c.HB):
        sl = 2.0 ** (-8.0 * (h + 1) / c.HB)
        ab[h, 0] = -sl * (a - cc)
        for j in range(4):
            kabs = 128 * j + cc
            vis = (kabs // 64) <= (a // 64)
            ab[h, 1 + j] = np.where(vis, -sl * np.abs(a - kabs), -1e30)
    return {"cst": cst, "rmask": rmask, "abias": ab}


def core_inputs(cfg, inputs, b, consts):
    c = cfg
    f = lambda a: np.ascontiguousarray(np.asarray(a, dtype=np.float32))
    m = {
        "x": f(inputs["x"][b]),
        "meta_tokens": f(inputs["meta_tokens"]),
        "norm1_g": f(inputs["norm1_g"]),
        "w_in": f(inputs["w_in"]),
        "hgrn_lb_raw": f(inputs["hgrn_lb_raw"]),
        "hgrn_out_g": f(inputs["hgrn_out_g"]),
        "q_norm_g": f(np.asarray(inputs["q_norm_g"]).reshape(c.DEPTH, 128)),
        "k_norm_g": f(np.asarray(inputs["k_norm_g"]).reshape(c.DEPTH, 128)),
        "diff_lambda": f(np.asarray(inputs["diff_lambda"]).reshape(c.DEPTH, 256)),
        "diff_out_g": f(inputs["diff_out_g"]),
        "w_out": f(inputs["w_out"]),
        "norm2_g": f(inputs["norm2_g"]),
        "w_mlp_up": f(inputs["w_mlp_up"]),
        "w_mlp_down": f(inputs["w_mlp_down"]),
    }
    m.update(consts)
    return m


_CACHE = {}


def kernel(**inputs):
    cfg = Cfg()
    if "nc" not in _CACHE:
        _CACHE["nc"] = build_program(cfg)
        _CACHE["consts"] = make_consts(cfg)
    nc = _CACHE["nc"]
    consts = _CACHE["consts"]
    B = np.asarray(inputs["x"]).shape[0]
    n_cores = 8
    in_maps = [core_inputs(cfg, inputs, i % B, consts) for i in range(n_cores)]
    res = run_bass_kernel_spmd(nc, in_maps, core_ids=list(range(n_cores)))
    outs = [np.asarray(res.results[i]["out"], dtype=np.float32) for i in range(B)]
    return np.stack(outs, axis=0)
```
